# Optimizing a Trainium2 kernel written in Bass

```python
import jax, jax.numpy as jnp
from jax import lax
import numpy as np

D_MODEL = 2048
BATCH = 8
SEQ = 2048
DEPTH = 1

CHUNK = 64
N_META = 16
D_CONF = D_MODEL // 2
D_SHORT = D_MODEL // 2
CONF_KERNEL = 31
SHORT_KERNEL = 3
D_FF = 4 * D_MODEL
IN_COLS = 2 * D_CONF + 3 * D_SHORT + 2 * D_MODEL
RMS_EPS = 1e-6
LN_EPS = 1e-5

kernel_name = "hybrid_gated_conformer_shortconv_block"


def rms_norm(x, g):
    xf = x.astype(jnp.float32)
    y = xf * lax.rsqrt(jnp.mean(xf * xf, axis=-1, keepdims=True) + RMS_EPS)
    return (y * g.astype(jnp.float32)).astype(x.dtype)


def layer_norm(x, g, b):
    xf = x.astype(jnp.float32)
    mu = jnp.mean(xf, axis=-1, keepdims=True)
    var = jnp.mean(jnp.square(xf - mu), axis=-1, keepdims=True)
    y = (xf - mu) * lax.rsqrt(var + LN_EPS)
    return (y * g.astype(jnp.float32) + b.astype(jnp.float32)).astype(x.dtype)


def causal_depthwise_conv(x, w, b=None):
    k = w.shape[0]
    y = lax.conv_general_dilated(
        x, w[:, None, :].astype(x.dtype),
        window_strides=(1,),
        padding=[(k - 1, 0)],
        dimension_numbers=("NWC", "WIO", "NWC"),
        feature_group_count=x.shape[-1])
    if b is not None:
        y = y + b.astype(x.dtype)
    return y


def setup_inputs(seed: int = 0) -> dict:
    key = jax.random.key(seed)
    ks = jax.random.split(key, 24)
    f32 = jnp.float32

    def nrm(k, shape, scale):
        return jax.random.normal(k, shape, f32) * scale

    def gain(k, shape):
        return 1.0 + 0.05 * jax.random.normal(k, shape, f32)

    return {
        "x": jax.random.normal(ks[0], (BATCH, SEQ, D_MODEL), f32),
        "meta": nrm(ks[1], (N_META, D_MODEL), 1.0),
        "g_pre_mix": gain(ks[2], (DEPTH, D_MODEL)),
        "w_in": nrm(ks[3], (DEPTH, D_MODEL, IN_COLS), D_MODEL ** -0.5),
        "b_gates": nrm(ks[4], (DEPTH, 2 * D_MODEL), 0.1),
        "conf_dw_w": nrm(ks[5], (DEPTH, CONF_KERNEL, D_CONF), CONF_KERNEL ** -0.5),
        "conf_dw_b": nrm(ks[6], (DEPTH, D_CONF), 0.02),
        "conf_ln_g": gain(ks[7], (DEPTH, D_CONF)),
        "conf_ln_b": nrm(ks[8], (DEPTH, D_CONF), 0.02),
        "conf_w_pw": nrm(ks[9], (DEPTH, D_CONF, D_MODEL), D_CONF ** -0.5),
        "short_dw_w": nrm(ks[10], (DEPTH, SHORT_KERNEL, D_SHORT), SHORT_KERNEL ** -0.5),
        "short_w_out": nrm(ks[11], (DEPTH, D_SHORT, D_MODEL), D_SHORT ** -0.5),
        "w_o": nrm(ks[12], (DEPTH, D_MODEL, D_MODEL), D_MODEL ** -0.5),
        "g_post_mix": gain(ks[13], (DEPTH, D_MODEL)),
        "g_pre_mlp": gain(ks[14], (DEPTH, D_MODEL)),
        "w_up": nrm(ks[15], (DEPTH, D_MODEL, D_FF), D_MODEL ** -0.5),
        "w_down": nrm(ks[16], (DEPTH, D_FF, D_MODEL), D_FF ** -0.5),
        "g_post_mlp": gain(ks[17], (DEPTH, D_MODEL)),
    }


def reference(x, meta, g_pre_mix, w_in, b_gates, conf_dw_w, conf_dw_b, conf_ln_g,
              conf_ln_b, conf_w_pw, short_dw_w, short_w_out, w_o, g_post_mix,
              g_pre_mlp, w_up, w_down, g_post_mlp):
    bsz = x.shape[0]
    meta_b = jnp.broadcast_to(meta.astype(x.dtype)[None], (bsz, N_META, D_MODEL))
    h = jnp.concatenate([meta_b, x], axis=1)

    for l in range(DEPTH):
        n = rms_norm(h, g_pre_mix[l])
        proj = jnp.einsum("btd,dc->btc", n, w_in[l])
        o1 = 2 * D_CONF
        o2 = o1 + 3 * D_SHORT
        u_a = proj[..., :o1]
        u_b = proj[..., o1:o2]
        gates = jax.nn.sigmoid(proj[..., o2:] + b_gates[l])
        gate_a = gates[..., :D_MODEL]
        gate_b = gates[..., D_MODEL:]

        a_val, a_gate = jnp.split(u_a, 2, axis=-1)
        a = a_val * jax.nn.sigmoid(a_gate)
        a = causal_depthwise_conv(a, conf_dw_w[l], conf_dw_b[l])
        a = jax.nn.silu(layer_norm(a, conf_ln_g[l], conf_ln_b[l]))
        y_a = jnp.einsum("btc,cd->btd", a, conf_w_pw[l])

        b_g, c_g, v = jnp.split(u_b, 3, axis=-1)
        s = b_g * causal_depthwise_conv(c_g * v, short_dw_w[l])
        y_b = jnp.einsum("btc,cd->btd", s, short_w_out[l])

        m = gate_a * y_a + gate_b * y_b
        mix = jnp.einsum("btd,de->bte", m, w_o[l])
        h = h + rms_norm(mix, g_post_mix[l])

        n2 = rms_norm(h, g_pre_mlp[l])
        f = jnp.square(jax.nn.relu(jnp.einsum("btd,df->btf", n2, w_up[l])))
        f = jnp.einsum("btf,fd->btd", f, w_down[l])
        h = h + rms_norm(f, g_post_mlp[l])

    return h[:, N_META:, :]
```

```python
import itertools
from contextlib import ExitStack

import numpy as np
import concourse.bass as bass
import concourse.mybir as mybir
from concourse.bass_utils import run_bass_kernel_spmd

F32 = mybir.dt.float32
BF16 = mybir.dt.bfloat16
AF = mybir.ActivationFunctionType
ALU = mybir.AluOpType

D = 2048
SEQ = 2048
NMETA = 16
DC = 1024
DFF = 8192
INC = 9216
TT = 512
NT = SEQ // TT
RMS_EPS = 1e-6
LN_EPS = 1e-5
KIB = 1024
N_CORES = 8


class V:
    __slots__ = ("ap", "blk")

    def __init__(self, ap, blk):
        self.ap = ap
        self.blk = blk


def _ranges(off, dims):
    dims = [(s, c) for s, c in dims if c > 1 and s != 0]
    if not dims:
        return [(off, off + 1)]
    if dims[-1][0] == 1:
        inner = dims[-1][1]
        outer = dims[:-1]
    else:
        inner = 1
        outer = dims
    n = 1
    for _, c in outer:
        n *= c
    if n > 512:
        hi = off + sum(s * (c - 1) for s, c in dims) + 1
        return [(off, hi)]
    out = []
    for idx in itertools.product(*[range(c) for _, c in outer]):
        o = off + sum(i * s for i, (s, _) in zip(idx, outer))
        out.append((o, o + inner))
    return out


class Buf:
    def __init__(self, nc, name, shape, dtype, off):
        self.t = nc.alloc_sbuf_tensor_at(name, list(shape), dtype, offset=off)
        self.ap = self.t.ap()
        self.off = off
        self.esz = 4 if dtype == F32 else 2
        self.pstep = int(np.prod(shape[1:]))
        self.nbytes = self.pstep * self.esz

    def __getitem__(self, idx):
        v = self.ap[idx]
        fo = v.offset % self.pstep
        dims = list(v.ap)[1:]
        blk = []
        for lo, hi in _ranges(fo, dims):
            b0 = (self.off + lo * self.esz) // 64
            b1 = (self.off + hi * self.esz - 1) // 64
            blk.extend(("s", b) for b in range(b0, b1 + 1))
        return V(v, blk)


class Sched:
    ENG = ("pe", "act", "dve", "pool", "sp")

    def __init__(self):
        self.ops = {e: [] for e in self.ENG}
        self.cnt = {}
        self.waited = {}
        self.lastw = {}
        self.readers = {}

    @staticmethod
    def _blks(items):
        for it in items:
            if isinstance(it, V):
                yield from it.blk
            else:
                yield it

    def _deps(self, eng, reads, writes, selfsync):
        deps = {}

        def add(tok):
            if tok is None:
                return
            k, v = tok
            if k == eng and not selfsync:
                return
            if deps.get(k, 0) < v:
                deps[k] = v

        for b in reads:
            add(self.lastw.get(b))
            if b[0] == "p":
                r = self.readers.get(b)
                if r:
                    for k, v in r.items():
                        add((k, v))
        for b in writes:
            add(self.lastw.get(b))
            r = self.readers.get(b)
            if r:
                for k, v in r.items():
                    add((k, v))
        waits = []
        for k, v in deps.items():
            if self.waited.get((eng, k), 0) < v:
                self.waited[(eng, k)] = v
                waits.append((k, v))
        return waits

    def _commit(self, tok, reads, writes):
        k, v = tok
        for b in reads:
            r = self.readers.setdefault(b, {})
            if r.get(k, 0) < v:
                r[k] = v
        for b in writes:
            self.lastw[b] = tok
            self.readers[b] = {}

    def op(self, eng, fns, reads, writes, selfsync=False):
        if callable(fns):
            fns = [fns]
        reads = list(self._blks(reads))
        writes = list(self._blks(writes))
        waits = self._deps(eng, reads, writes, selfsync)
        self.cnt[eng] = self.cnt.get(eng, 0) + 1
        tok = (eng, self.cnt[eng])
        self._commit(tok, reads, writes)
        self.ops[eng].append((waits, fns, eng, 1, False))
        return tok

    def dma(self, eng, semkey, fns, reads, writes):
        if callable(fns):
            fns = [fns]
        reads = list(self._blks(reads))
        writes = list(self._blks(writes))
        waits = self._deps(eng, reads, writes, True)
        self.cnt[semkey] = self.cnt.get(semkey, 0) + 16 * len(fns)
        tok = (semkey, self.cnt[semkey])
        self._commit(tok, reads, writes)
        self.ops[eng].append((waits, fns, semkey, 16, True))
        return tok


class _Stop(Exception):
    pass


def build_nc(stop=None, conv_window=5):
    nc = bass.Bass("TRN2", target_bir_lowering=False)
    dumps = []

    def checkpoint(name, views):
        if stop == name:
            dumps.extend(views)
            raise _Stop()

    def din(name, shape):
        return nc.dram_tensor(name, list(shape), F32, kind="ExternalInput").ap()

    x_d = din("x", [SEQ, D])
    meta_d = din("meta", [NMETA, D])
    w_in_d = din("w_in", [D, INC])
    pw_d = din("conf_w_pw", [DC, D])
    so_d = din("short_w_out", [DC, D])
    wo_d = din("w_o", [D, D])
    up_d = din("w_up", [D, DFF])
    dn_d = din("w_down", [DFF, D])
    gpm_d = din("gcol_pre_mix", [128, 16])
    gpl_d = din("gcol_pre_mlp", [128, 16])
    gpost_mix_d = din("g_post_mix", [1, D])
    gpost_mlp_d = din("g_post_mlp", [1, D])
    bgc_d = din("bgates_col", [128, 32])
    cw_d = din("conf_w_col", [128, 8 * 31])
    cb_d = din("conf_b_col", [128, 8])
    lng_d = din("conf_lng_col", [128, 8])
    lnb_d = din("conf_lnb_col", [128, 8])
    sw_d = din("short_w_col", [128, 8 * 3])
    ident_d = din("ident", [128, 128])
    out_d = nc.dram_tensor("out", [SEQ, D], F32, kind="ExternalOutput").ap()

    s_win = nc.dram_tensor("s_win", [D, INC], BF16, kind="Internal").ap()
    s_pw = nc.dram_tensor("s_pw", [DC, D], BF16, kind="Internal").ap()
    s_so = nc.dram_tensor("s_so", [DC, D], BF16, kind="Internal").ap()
    s_wo = nc.dram_tensor("s_wo", [D, D], BF16, kind="Internal").ap()
    s_up = nc.dram_tensor("s_up", [D, DFF], BF16, kind="Internal").ap()
    s_dn = nc.dram_tensor("s_dn", [DFF, D], BF16, kind="Internal").ap()

    def B(name, shape, dtype, off):
        return Buf(nc, name, shape, dtype, off)

    wbuf = [B("wbuf0", [128, 16, 512], BF16, 16 * KIB), B("wbuf1", [128, 16, 512], BF16, 32 * KIB)]
    h = B("h", [128, 4, 2048], F32, 48 * KIB)
    gbc = B("gbc", [128, 2048], F32, 80 * KIB)
    cbuf = B("c", [128, 8, 512], F32, 88 * KIB)
    nT = B("nT", [128, 16, 512], BF16, 104 * KIB)
    park3 = B("park3", [128, 4, 2048], F32, 88 * KIB)
    R2 = 120 * KIB
    fT = B("fT", [128, 64, 512], BF16, R2)
    mT = B("mT", [128, 16, 512], BF16, R2)
    o = R2 + 16 * KIB
    a_bf = B("a_bf", [128, 8, 544], BF16, o)
    o += 8 * 544 * 2
    cv_bf = B("cv_bf", [128, 8, 520], BF16, o)
    o += 8 * 520 * 2
    aln = B("aln", [128, 8, 512], BF16, o)
    o += 8192
    s_bf = B("s_bf", [128, 8, 512], BF16, o)
    o += 8192
    park1 = B("park1", [128, 4, 2048], F32, R2 + 16 * KIB)
    assert o % 64 == 0 and o >= R2 + 48 * KIB
    parkB = B("parkB", [128, 4, 512], F32, o)
    o += 8192
    cbf_t = [B("cbf0", [128, 512], BF16, o), B("cbf1", [128, 512], BF16, o + 1024)]
    csq_t = [B("csq0", [128, 512], BF16, o + 2048), B("csq1", [128, 512], BF16, o + 3072)]
    o += 4096
    assert o <= R2 + 64 * KIB
    xs = B("xs", [128, 2048], F32, 184 * KIB)
    parkA = B("parkA", [128, 4, 512], F32, 184 * KIB)
    xs2 = B("xs2", [128, 2048], F32, parkB.off)
    xs_bufs = [xs, xs2]
    dbuf = [B("dbuf0", [128, 31, 128], BF16, 192 * KIB), B("dbuf1", [128, 31, 128], BF16, 192 * KIB + 7936)]
    o = 192 * KIB + 2 * 7936
    assert o % 64 == 0

    def small(name, shape, dtype):
        nonlocal o
        b = B(name, shape, dtype, o)
        o += (b.nbytes + 63) // 64 * 64
        return b

    lnv = small("lnv", [128, 512], F32)
    junk = B("junk", [128, 1024], BF16, lnv.off)
    tmpA = [small("tmpA0", [128, 512], F32), small("tmpA1", [128, 512], F32)]
    tmpB = [small("tmpB0", [128, 512], F32), small("tmpB1", [128, 512], F32)]
    ident = small("ident", [128, 128], F32)
    ones_bf = small("ones_bf", [128, 128], BF16)
    gpm = small("gpm", [128, 16], F32)
    gpl = small("gpl", [128, 16], F32)
    bgc = small("bgc", [128, 32], F32)
    cwc = small("cwc", [128, 8, 31], F32)
    cbc = small("cbc", [128, 8], F32)
    lngc = small("lngc", [128, 8], F32)
    lnbc = small("lnbc", [128, 8], F32)
    swc = small("swc", [128, 8, 3], F32)
    hist_a = small("hist_a", [128, 8, 32], BF16)
    hist_cv = small("hist_cv", [128, 8, 2], BF16)
    d3buf = [small("d3b0", [128, 3, 128], BF16), small("d3b1", [128, 3, 128], BF16)]
    neghalf = small("neghalf", [128, 1], F32)
    st = small("st", [128, 64], F32)
    assert o <= 224 * KIB, o

    pst = nc.alloc_psum_tensor("ps", [128, 8, 512], F32).ap()

    def PS(b, lo=0, hi=512, np_=128):
        return V(pst[0:np_, b, lo:hi], [("p", b)])

    S = Sched()
    bank_ctr = [0]

    def alloc_bank():
        b = bank_ctr[0] % 8
        bank_ctr[0] += 1
        return b

    def ACT(fn, r, w, **kw):
        return S.op("act", fn, r, w, **kw)

    def DVE(fn, r, w, **kw):
        return S.op("dve", fn, r, w, **kw)

    def POOL(fn, r, w, **kw):
        return S.op("dve", fn, r, w, **kw)

    def PE(fns, r, w, **kw):
        return S.op("pe", fns, r, w, **kw)

    C_SS = 0
    C_MS = 8
    C_RSTD = 12
    C_SSQ = 16
    C_MS2 = 32
    C_RSTD2 = 36

    cl = []
    for dst, src in ((ident, ident_d), (gpm, gpm_d), (gpl, gpl_d), (bgc, bgc_d), (cbc, cb_d),
                     (lngc, lng_d), (lnbc, lnb_d)):
        cl.append((dst[:, :], src))
    cl.append((cwc[:, :, :], cw_d.rearrange("p (j k) -> p j k", k=31)))
    cl.append((swc[:, :, :], sw_d.rearrange("p (j k) -> p j k", k=3)))
    S.dma("pool", "const", [(lambda e, d=d, s=s: e.dma_start(out=d.ap, in_=s)) for d, s in cl],
          [], [d for d, _ in cl])

    conv_id = [0]
    conv_keys = []
    pending = []

    def convert(pieces, key):
        pending.append((pieces, key))

    conv_desc = []

    def issue_conv(n):
        budget = 128 * conv_window
        for _ in range(n):
            if not pending:
                return
            pieces, key = pending.pop(0)
            i = conv_id[0]
            conv_id[0] += 1
            nd = sum(int(d.shape[0]) for d, _ in pieces) // 16
            conv_desc.append(nd)
            tot = nd
            j = i - 1
            while j >= 0 and tot + conv_desc[j] <= budget:
                tot += conv_desc[j]
                j -= 1
            thr = [conv_keys[j]] if j >= 0 else []
            conv_keys.append(key)
            S.dma("pool", f"cv{i}", [(lambda e, d=d, s=s: e.dma_start(out=d, in_=s)) for d, s in pieces], thr, [key])

    POOL(lambda e: e.memset(h.ap[:, 0, :], 0.0), [], [h[:, 0, :]])
    S.dma("act", "x0", lambda e: e.dma_start(out=h[0:NMETA, 0, :].ap, in_=meta_d), [], [h[:, 0, :]])

    def halves(dst, src_):
        return [(dst[:, 0:1024], src_[:, 0:1024]), (dst[:, 1024:2048], src_[:, 1024:2048])]

    for cb in (1, 0, 3, 4, 2, 5, 6, 7, 8):
        c0, c1 = cb * 1024, (cb + 1) * 1024
        convert([(s_win[:, c0:c1], w_in_d[:, c0:c1])], ("d", "win", cb))
    convert(halves(s_pw, pw_d), ("d", "pw"))
    convert(halves(s_so, so_d), ("d", "so"))
    convert(halves(s_wo, wo_d), ("d", "wo"))
    for cb in range(4):
        convert([(s_up[:, cb * 2048:(cb + 1) * 2048], up_d[:, cb * 2048:(cb + 1) * 2048])], ("d", "up", cb))
    for rb in range(4):
        convert(halves(s_dn[rb * 2048:(rb + 1) * 2048, :], dn_d[rb * 2048:(rb + 1) * 2048, :]), ("d", "dn", rb))

    issue_conv(1000)
    POOL(lambda e: e.memset(neghalf.ap, -0.5), [], [neghalf[:, :]])
    POOL(lambda e: e.memset(ones_bf.ap, 1.0 / DC), [], [ones_bf[:, :]])
    POOL(lambda e: e.memset(hist_a.ap, 0.0), [], [hist_a[:, :, :]])
    POOL(lambda e: e.memset(hist_cv.ap, 0.0), [], [hist_cv[:, :, :]])

    slot_ctr = [0]
    win_v = s_win.rearrange("(k p) c -> p k c", p=128)
    pw_v = s_pw.rearrange("(k p) c -> p k c", p=128)
    so_v = s_so.rearrange("(k p) c -> p k c", p=128)
    wo_v = s_wo.rearrange("(k p) c -> p k c", p=128)
    up_v = s_up.rearrange("(k p) c -> p k c", p=128)
    dn_v = s_dn.rearrange("(k p) c -> p k c", p=128)

    def load_unit(parts, keys):
        slot = slot_ctr[0] % 2
        slot_ctr[0] += 1
        wb = wbuf[slot]
        fns = [(lambda e, k0=k0, k1=k1, src=src: e.dma_start(out=wb.ap[:, k0:k1, :], in_=src))
               for k0, k1, src in parts]
        S.dma("sp", f"w{slot}", fns, keys, [wb[:, :, :]])
        return slot

    def load_win(col0):
        return load_unit([(0, 16, win_v[:, :, col0:col0 + 512])], [("d", "win", col0 // 1024)])

    def rstd_from(np_, ss_cols, ms_col, rstd_col, eps, denom):
        c0, n = ss_cols
        ms = st[0:np_, ms_col:ms_col + 1]
        DVE(lambda e: e.tensor_reduce(out=ms.ap, in_=st.ap[0:np_, c0:c0 + n],
                                      axis=mybir.AxisListType.X, op=ALU.add),
            [st[0:np_, c0:c0 + n]], [ms])
        DVE(lambda e: e.tensor_scalar(out=ms.ap, in0=ms.ap, scalar1=1.0 / denom, scalar2=eps,
                                      op0=ALU.mult, op1=ALU.add), [ms], [ms], selfsync=True)
        rs = st[0:np_, rstd_col:rstd_col + 1]
        ACT(lambda e: e.activation(out=ms.ap, in_=ms.ap, func=AF.Sqrt), [ms], [ms])
        DVE(lambda e: e.reciprocal(out=rs.ap, in_=ms.ap), [ms], [rs])
        return rs

    def norm_T(hrow, np_, s, gcol, col0):
        for hh in range(2):
            cc = C_SS + s * 2 + hh
            ACT(lambda e, hh=hh, cc=cc: e.activation(out=junk.ap[0:np_, :], in_=hrow.ap[:, hh * 1024:(hh + 1) * 1024],
                                                     func=AF.Square, accum_out=st.ap[0:np_, cc:cc + 1]),
                [hrow], [junk[0:np_, :], st[0:np_, cc:cc + 1]])
        checkpoint("nt1", [("st", st[:, :])])
        rs = rstd_from(np_, (C_SS + s * 2, 2), C_MS + s, C_RSTD + s, RMS_EPS, float(D))
        checkpoint("nt2", [("st", st[:, :])])
        xsb = xs_bufs[s % 2]
        xv = xsb[0:np_, :]
        ACT(lambda e: e.activation(out=xv.ap, in_=hrow.ap, func=AF.Identity, scale=rs.ap), [hrow, rs], [xv])
        checkpoint("nt3", [("xs", xsb[:, :])])
        for kb in range(4):
            b = alloc_bank()
            fns = [(lambda e, k=kb * 4 + i, i=i, b=b: e.transpose(out=pst[:, b, i * 128:i * 128 + np_],
                                                            in_=xsb.ap[0:np_, k * 128:(k + 1) * 128],
                                                            identity=ident.ap[0:np_, 0:np_]))
                   for i in range(4)]
            PE(fns, [xv, ident[:, :]], [("p", b)])
            checkpoint("nt4", [("st", st[:, :])])
            for i in range(4):
                k = kb * 4 + i
                dst = nT[:, k, col0:col0 + np_]
                src = PS(b, i * 128, i * 128 + np_)
                if kb % 2 == 0:
                    ACT(lambda e, dst=dst, src=src, k=k: e.activation(out=dst.ap, in_=src.ap, func=AF.Identity,
                                                                      scale=gcol.ap[:, k:k + 1]),
                        [src, gcol[:, k:k + 1]], [dst])
                    checkpoint("nt5", [("st", st[:, :])])
                else:
                    DVE(lambda e, dst=dst, src=src, k=k: e.tensor_scalar(out=dst.ap, in0=src.ap,
                                                                         scalar1=gcol.ap[:, k:k + 1], scalar2=None,
                                                                         op0=ALU.mult),
                        [src, gcol[:, k:k + 1]], [dst])

    def formF_chunk(slot, k0, kn, jj, rhs_buf, rk0, ncols):
        b = alloc_bank()
        wb = wbuf[slot]
        fns = [(lambda e, k=k: e.matmul(out=pst[:, b, 0:ncols], lhsT=wb.ap[:, k0 + k, jj * 128:(jj + 1) * 128],
                                        rhs=rhs_buf.ap[:, rk0 + k, 0:ncols], start=(k == 0), stop=(k == kn - 1)))
               for k in range(kn)]
        PE(fns, [wb[:, k0:k0 + kn, jj * 128:(jj + 1) * 128], rhs_buf[:, rk0:rk0 + kn, 0:ncols]], [("p", b)])
        return b

    def build_d3(j):
        d3 = d3buf[j % 2]
        POOL(lambda e: e.tensor_tensor(out=d3.ap, in0=ident.ap.unsqueeze(1).broadcast_to([128, 3, 128]),
                                       in1=swc.ap[:, j, :].unsqueeze(2).broadcast_to([128, 3, 128]), op=ALU.mult),
             [ident[:, :], swc[:, j, :]], [d3[:, :, :]])

    def build_d31(j):
        dd = dbuf[j % 2]
        POOL(lambda e: e.tensor_tensor(out=dd.ap, in0=ident.ap.unsqueeze(1).broadcast_to([128, 31, 128]),
                                       in1=cwc.ap[:, j, :].unsqueeze(2).broadcast_to([128, 31, 128]), op=ALU.mult),
             [ident[:, :], cwc[:, j, :]], [dd[:, :, :]])

    def win_part1(ncols, is_meta):
        for half in range(2):
            slot = load_win(1024 + half * 512)
            for jj in range(4):
                b = formF_chunk(slot, 0, 16, jj, nT, 0, ncols)
                src = PS(b, 0, ncols)
                dst = parkA[:, jj, 0:ncols]
                ACT(lambda e, dst=dst, src=src: e.activation(out=dst.ap, in_=src.ap, func=AF.Sigmoid), [src], [dst])
            slot = load_win(0 + half * 512)
            for jj in range(4):
                j = half * 4 + jj
                b = formF_chunk(slot, 0, 16, jj, nT, 0, ncols)
                src = PS(b, 0, ncols)
                pk = parkA[:, jj, 0:ncols]
                dst = hist_a[:, j, 14:30] if is_meta else a_bf[:, j, 30:542]
                DVE(lambda e, dst=dst, src=src, pk=pk: e.tensor_tensor(out=dst.ap, in0=src.ap, in1=pk.ap, op=ALU.mult),
                    [src, pk], [dst])
        for half in range(2):
            if not is_meta:
                slot = load_win(2048 + half * 512)
                for jj in range(4):
                    b = formF_chunk(slot, 0, 16, jj, nT, 0, ncols)
                    src = PS(b, 0, ncols)
                    dst = parkB[:, jj, 0:ncols]
                    ACT(lambda e, dst=dst, src=src: e.activation(out=dst.ap, in_=src.ap, func=AF.Copy), [src], [dst])
            slot = load_win(3072 + half * 512)
            for jj in range(4):
                b = formF_chunk(slot, 0, 16, jj, nT, 0, ncols)
                src = PS(b, 0, ncols)
                dst = parkA[:, jj, 0:ncols]
                ACT(lambda e, dst=dst, src=src: e.activation(out=dst.ap, in_=src.ap, func=AF.Copy), [src], [dst])
            slot = load_win(4096 + half * 512)
            for jj in range(4):
                j = half * 4 + jj
                b = formF_chunk(slot, 0, 16, jj, nT, 0, ncols)
                if is_meta:
                    src = PS(b, 14, 16)
                    pk = parkA[:, jj, 14:16]
                    dst = hist_cv[:, j, 0:2]
                else:
                    src = PS(b, 0, ncols)
                    pk = parkA[:, jj, 0:ncols]
                    dst = cv_bf[:, j, 2:514]
                DVE(lambda e, dst=dst, src=src, pk=pk: e.tensor_tensor(out=dst.ap, in0=src.ap, in1=pk.ap, op=ALU.mult),
                    [src, pk], [dst])
            if not is_meta:
                for jj in range(4):
                    j = half * 4 + jj
                    build_d3(j)
                    d3 = d3buf[j % 2]
                    b = alloc_bank()
                    fns = [(lambda e, k=k, j=j, d3=d3, b=b: e.matmul(out=pst[:, b, :], lhsT=d3.ap[:, k, :],
                                                                 rhs=cv_bf.ap[:, j, k:k + 512],
                                                                 start=(k == 0), stop=(k == 2))) for k in range(3)]
                    PE(fns, [d3[:, :, :], cv_bf[:, j, 0:514]], [("p", b)])
                    src = PS(b)
                    pk = parkB[:, jj, :]
                    dst = s_bf[:, j, :]
                    DVE(lambda e, dst=dst, src=src, pk=pk: e.tensor_tensor(out=dst.ap, in0=src.ap, in1=pk.ap,
                                                                           op=ALU.mult), [src, pk], [dst])

    def prologue():
        checkpoint("init", [("cwc", cwc[:, :, :]), ("ones", ones_bf[:, :])])
        norm_T(h[:, 0, :], 128, 0, gpm, 0)
        issue_conv(1)
        checkpoint("pro_norm", [("nT", nT[:, :, 0:16]), ("st", st[:, :])])
        win_part1(NMETA, True)
        issue_conv(1)
        checkpoint("pro_win", [("hist_a", hist_a[:, :, :]), ("hist_cv", hist_cv[:, :, :])])

    def tile(t):
        for s in range(4):
            r0 = t * TT + s * 128
            S.dma("act", f"x{s}", lambda e, s=s, r0=r0: e.dma_start(out=h.ap[:, s, :], in_=x_d[r0:r0 + 128, :]),
                  [], [h[:, s, :]])
        POOL(lambda e: e.tensor_copy(out=a_bf.ap[:, :, 0:30], in_=hist_a.ap[:, :, 0:30]),
             [hist_a[:, :, 0:30]], [a_bf[:, :, 0:30]])
        POOL(lambda e: e.tensor_copy(out=cv_bf.ap[:, :, 0:2], in_=hist_cv.ap[:, :, 0:2]),
             [hist_cv[:, :, :]], [cv_bf[:, :, 0:2]])
        for s in range(4):
            norm_T(h[:, s, :], 128, s, gpm, s * 128)

        issue_conv(2)
        if t == 0:
            checkpoint("t0_norm", [("nT", nT[:, :, :])])
        win_part1(TT, False)
        issue_conv(2)
        if t == 0:
            checkpoint("t0_win1", [("a_bf", a_bf[:, :, :]), ("cv_bf", cv_bf[:, :, :]), ("s_bf", s_bf[:, :, :])])

        POOL(lambda e: e.tensor_copy(out=hist_a.ap[:, :, 0:30], in_=a_bf.ap[:, :, 512:542]),
             [a_bf[:, :, 512:542]], [hist_a[:, :, 0:30]], selfsync=True)
        POOL(lambda e: e.tensor_copy(out=hist_cv.ap[:, :, 0:2], in_=cv_bf.ap[:, :, 512:514]),
             [cv_bf[:, :, 512:514]], [hist_cv[:, :, :]], selfsync=True)

        S1 = alloc_bank()
        S2 = alloc_bank()
        build_d31(0)
        build_d31(1)

        def stats_mm(j):
            cb_, cs_ = cbf_t[j % 2], csq_t[j % 2]
            PE(lambda e: e.matmul(out=pst[:, S1, :], lhsT=ones_bf.ap, rhs=cb_.ap, start=(j == 0), stop=(j == 7)),
               [ones_bf[:, :], cb_[:, :]], [("p", S1)])
            PE(lambda e: e.matmul(out=pst[:, S2, :], lhsT=ones_bf.ap, rhs=cs_.ap, start=(j == 0), stop=(j == 7)),
               [ones_bf[:, :], cs_[:, :]], [("p", S2)])

        for j in range(8):
            dd = dbuf[j % 2]
            b = alloc_bank()
            while b in (S1, S2):
                b = alloc_bank()
            fns = [(lambda e, k=k, j=j, dd=dd, b=b: e.matmul(out=pst[:, b, :], lhsT=dd.ap[:, k, :],
                                                             rhs=a_bf.ap[:, j, k:k + 512],
                                                             start=(k == 0), stop=(k == 30))) for k in range(31)]
            PE(fns, [dd[:, :, :], a_bf[:, j, 0:542]], [("p", b)])
            if j + 2 < 8:
                build_d31(j + 2)
            src = PS(b)
            cbj = cbc[:, j:j + 1]
            ACT(lambda e, src=src, j=j, cbj=cbj: e.activation(out=cbuf.ap[:, j, :], in_=src.ap, func=AF.Identity,
                                                              bias=cbj.ap), [src, cbj], [cbuf[:, j, :]])
            cb_, cs_ = cbf_t[j % 2], csq_t[j % 2]
            DVE(lambda e, j=j, cb_=cb_: e.tensor_copy(out=cb_.ap, in_=cbuf.ap[:, j, :]),
                [cbuf[:, j, :]], [cb_[:, :]])
            ACT(lambda e, src=src, cbj=cbj, cs_=cs_: e.activation(out=cs_.ap, in_=src.ap, func=AF.Square,
                                                                  bias=cbj.ap), [src, cbj], [cs_[:, :]])
            if j >= 1:
                stats_mm(j - 1)
        stats_mm(7)

        s1v, s2v = PS(S1), PS(S2)
        ACT(lambda e: e.activation(out=lnv.ap, in_=s1v.ap, func=AF.Square), [s1v], [lnv[:, :]])
        DVE(lambda e: e.tensor_tensor(out=lnv.ap, in0=s2v.ap, in1=lnv.ap, op=ALU.subtract),
            [s2v, lnv[:, :]], [lnv[:, :]])
        DVE(lambda e: e.tensor_scalar(out=lnv.ap, in0=lnv.ap, scalar1=LN_EPS, scalar2=0.0, op0=ALU.add, op1=ALU.max),
            [lnv[:, :]], [lnv[:, :]], selfsync=True)
        ACT(lambda e: e.activation(out=lnv.ap, in_=lnv.ap, func=AF.Sqrt), [lnv[:, :]], [lnv[:, :]])
        DVE(lambda e: e.reciprocal(out=lnv.ap, in_=lnv.ap), [lnv[:, :]], [lnv[:, :]])

        def ln_apply(j):
            ta, tb = tmpA[j % 2], tmpB[j % 2]
            DVE(lambda e: e.tensor_tensor(out=ta.ap, in0=cbuf.ap[:, j, :], in1=s1v.ap, op=ALU.subtract),
                [cbuf[:, j, :], s1v], [ta[:, :]])
            DVE(lambda e: e.tensor_tensor(out=tb.ap, in0=ta.ap, in1=lnv.ap, op=ALU.mult),
                [ta[:, :], lnv[:, :]], [tb[:, :]], selfsync=True)
            ACT(lambda e: e.activation(out=tb.ap, in_=tb.ap, func=AF.Identity, scale=lngc.ap[:, j:j + 1],
                                       bias=lnbc.ap[:, j:j + 1]),
                [tb[:, :], lngc[:, j:j + 1], lnbc[:, j:j + 1]], [tb[:, :]])
            ACT(lambda e: e.activation(out=ta.ap, in_=tb.ap, func=AF.Sigmoid), [tb[:, :]], [ta[:, :]], selfsync=True)
            POOL(lambda e: e.tensor_tensor(out=aln.ap[:, j, :], in0=tb.ap, in1=ta.ap, op=ALU.mult),
                 [ta[:, :], tb[:, :]], [aln[:, j, :]])

        for j in range(8):
            ln_apply(j)

        issue_conv(2)
        if t == 0:
            checkpoint("t0_conv", [("c", cbuf[:, :, :]), ("aln", aln[:, :, :]), ("lnv", lnv[:, :])])
        for q in range(4):
            slot = load_win(5120 + q * 512)
            for jj in range(4):
                b = formF_chunk(slot, 0, 16, jj, nT, 0, TT)
                src = PS(b)
                col = bgc[:, q * 4 + jj:q * 4 + jj + 1]
                dst = parkA[:, jj, :]
                ACT(lambda e, dst=dst, src=src, col=col: e.activation(out=dst.ap, in_=src.ap, func=AF.Sigmoid,
                                                                      bias=col.ap), [src, col], [dst])
            slot = load_win(7168 + q * 512)
            for jj in range(4):
                b = formF_chunk(slot, 0, 16, jj, nT, 0, TT)
                src = PS(b)
                col = bgc[:, 16 + q * 4 + jj:16 + q * 4 + jj + 1]
                dst = parkB[:, jj, :]
                ACT(lambda e, dst=dst, src=src, col=col: e.activation(out=dst.ap, in_=src.ap, func=AF.Sigmoid,
                                                                      bias=col.ap), [src, col], [dst])
            slot = load_unit([(0, 8, pw_v[:, :, q * 512:(q + 1) * 512]), (8, 16, so_v[:, :, q * 512:(q + 1) * 512])],
                             [("d", "pw"), ("d", "so")])
            for jj in range(4):
                b = formF_chunk(slot, 0, 8, jj, aln, 0, TT)
                src = PS(b)
                pk = parkA[:, jj, :]
                DVE(lambda e, src=src, pk=pk: e.tensor_tensor(out=pk.ap, in0=src.ap, in1=pk.ap, op=ALU.mult),
                    [src, pk], [pk])
            for jj in range(4):
                b = formF_chunk(slot, 8, 8, jj, s_bf, 0, TT)
                src = PS(b)
                pk = parkB[:, jj, :]
                DVE(lambda e, src=src, pk=pk: e.tensor_tensor(out=pk.ap, in0=src.ap, in1=pk.ap, op=ALU.mult),
                    [src, pk], [pk])
                pa = parkA[:, jj, :]
                dst = mT[:, q * 4 + jj, :]
                POOL(lambda e, dst=dst, pa=pa, pk=pk: e.tensor_tensor(out=dst.ap, in0=pa.ap, in1=pk.ap, op=ALU.add),
                     [pa, pk], [dst], selfsync=True)

        issue_conv(2)
        if t == 0:
            checkpoint("t0_m", [("mT", mT[:, :, :])])
        def formT_stage(act_buf, nk_groups, wview, key_fn, gpost_d, park):
            S.dma("sp", "gbc", lambda e: e.dma_start(out=gbc.ap, in_=gpost_d.broadcast_to([128, D])),
                  [], [gbc[:, :]])
            for j in range(4):
                banks = [alloc_bank() for _ in range(4)]
                for kg in range(nk_groups):
                    slot = load_unit([(0, 16, wview[:, kg * 16:(kg + 1) * 16, j * 512:(j + 1) * 512])], [key_fn(kg)])
                    wb = wbuf[slot]
                    fns = []
                    for k in range(16):
                        for s in range(4):
                            fns.append(lambda e, k=k, s=s, kg=kg, wb=wb, banks=tuple(banks): e.matmul(
                                out=pst[:, banks[s], :], lhsT=act_buf.ap[:, kg * 16 + k, s * 128:(s + 1) * 128],
                                rhs=wb.ap[:, k, :], start=(kg == 0 and k == 0),
                                stop=(kg == nk_groups - 1 and k == 15)))
                    PE(fns, [wb[:, :, :], act_buf[:, kg * 16:(kg + 1) * 16, :]], [("p", bb) for bb in banks])
                for s in range(4):
                    src = PS(banks[s])
                    cc = C_SSQ + s * 4 + j
                    ACT(lambda e, src=src, cc=cc: e.activation(out=junk.ap[:, 0:512], in_=src.ap, func=AF.Square,
                                                               accum_out=st.ap[:, cc:cc + 1]),
                        [src], [junk[:, 0:512], st[:, cc:cc + 1]])
                    dst = park[:, s, j * 512:(j + 1) * 512]
                    gs = gbc[:, j * 512:(j + 1) * 512]
                    DVE(lambda e, dst=dst, src=src, gs=gs: e.tensor_tensor(out=dst.ap, in0=src.ap, in1=gs.ap,
                                                                           op=ALU.mult), [src, gs], [dst])
            for s in range(4):
                rs = rstd_from(128, (C_SSQ + s * 4, 4), C_MS2 + s, C_RSTD2 + s, RMS_EPS, float(D))
                hv = h[:, s, :]
                pv = park[:, s, :]
                DVE(lambda e, hv=hv, pv=pv, rs=rs: e.scalar_tensor_tensor(out=hv.ap, in0=pv.ap, scalar=rs.ap,
                                                                          in1=hv.ap, op0=ALU.mult, op1=ALU.add),
                    [pv, rs, hv], [hv], selfsync=True)

        formT_stage(mT, 1, wo_v, lambda kg: ("d", "wo"), gpost_mix_d, park1)
        issue_conv(2)
        if t == 0:
            checkpoint("t0_wo", [("h", h[:, :, :])])

        for s in range(4):
            norm_T(h[:, s, :], 128, s, gpl, s * 128)
        issue_conv(4)
        if t == 0:
            checkpoint("t0_n2", [("nT", nT[:, :, :])])
        for u in range(16):
            slot = load_unit([(0, 16, up_v[:, :, u * 512:(u + 1) * 512])], [("d", "up", u // 4)])
            for jj in range(4):
                ff = u * 4 + jj
                b = formF_chunk(slot, 0, 16, jj, nT, 0, TT)
                src = PS(b)
                xf = tmpA[ff % 2]
                ACT(lambda e, src=src, xf=xf: e.activation(out=xf.ap, in_=src.ap, func=AF.Copy), [src], [xf[:, :]])
                dst = fT[:, ff, :]
                DVE(lambda e, dst=dst, src=src, xf=xf: e.scalar_tensor_tensor(out=dst.ap, in0=src.ap, scalar=0.0,
                                                                              in1=xf.ap, op0=ALU.max, op1=ALU.mult),
                    [src, xf[:, :]], [dst])
        if t == 0:
            checkpoint("t0_up", [("fT", fT[:, :, :])])
        formT_stage(fT, 4, dn_v, lambda kg: ("d", "dn", kg), gpost_mlp_d, park3)
        if t == 0:
            checkpoint("t0_end", [("h", h[:, :, :]), ("st", st[:, :]), ("park3", park3[:, :, :])])

        for s in range(4):
            r0 = t * TT + s * 128
            S.dma("pool", f"o{s}", lambda e, s=s, r0=r0: e.dma_start(out=out_d[r0:r0 + 128, :], in_=h.ap[:, s, :]),
                  [h[:, s, :]], [("d", "out", t, s)])

    try:
        prologue()
        for t in range(NT):
            tile(t)
    except _Stop:
        pass
    dump_specs = []
    for name, v in dumps:
        shp = [128, int(np.prod(v.ap.shape[1:]))]
        dd = nc.dram_tensor("dbg_" + name, list(v.ap.shape), F32, kind="ExternalOutput").ap()
        dump_specs.append(name)
        S.dma("pool", "dbg_" + name, lambda e, dd=dd, v=v: e.dma_start(out=dd, in_=v.ap), [v], [("d", "dbg", name)])

    semkeys = sorted(S.cnt.keys())
    with ExitStack() as es:
        sems = {k: es.enter_context(nc.semaphore("sem_" + k)) for k in semkeys}
        block = es.enter_context(nc.Block())

        def replay(engname, final_waits=()):
            def run(e):
                for waits, fns, sk, inc, every in S.ops[engname]:
                    for k, v in waits:
                        e.wait_ge(sems[k], v)
                    inst = None
                    for fn in fns:
                        inst = fn(e)
                        if every:
                            inst.then_inc(sems[sk], inc)
                    if not every:
                        inst.then_inc(sems[sk], inc)
                for k in final_waits:
                    e.wait_ge(sems[k], S.cnt[k])
            return run

        block.tensor(replay("pe"))
        block.scalar(replay("act"))
        block.vector(replay("dve"))
        block.gpsimd(replay("pool", [k for k in semkeys if k.startswith("o") or k.startswith("dbg_")]))
        block.sync(replay("sp"))
    nc._dump_specs = dump_specs
    return nc


_NC_CACHE = {}


def _col(v, n):
    return np.ascontiguousarray(np.asarray(v, np.float32).reshape(n, 128).T)


def make_shared(meta, g_pre_mix, w_in, b_gates, conf_dw_w, conf_dw_b, conf_ln_g, conf_ln_b, conf_w_pw,
                short_dw_w, short_w_out, w_o, g_post_mix, g_pre_mlp, w_up, w_down, g_post_mlp):
    f = lambda a: np.ascontiguousarray(np.asarray(a, dtype=np.float32))
    cw = np.asarray(conf_dw_w, np.float32)[0]
    cw_col = np.ascontiguousarray(cw.T.reshape(8, 128, 31).transpose(1, 0, 2).reshape(128, 8 * 31))
    sw = np.asarray(short_dw_w, np.float32)[0]
    sw_col = np.ascontiguousarray(sw.T.reshape(8, 128, 3).transpose(1, 0, 2).reshape(128, 8 * 3))
    return {
        "meta": f(meta),
        "w_in": f(w_in[0]),
        "conf_w_pw": f(conf_w_pw[0]),
        "short_w_out": f(short_w_out[0]),
        "w_o": f(w_o[0]),
        "w_up": f(w_up[0]),
        "w_down": f(w_down[0]),
        "gcol_pre_mix": _col(g_pre_mix[0], 16),
        "gcol_pre_mlp": _col(g_pre_mlp[0], 16),
        "g_post_mix": f(g_post_mix[0]).reshape(1, D),
        "g_post_mlp": f(g_post_mlp[0]).reshape(1, D),
        "bgates_col": _col(b_gates[0], 32),
        "conf_w_col": cw_col,
        "conf_b_col": _col(conf_dw_b[0], 8),
        "conf_lng_col": _col(conf_ln_g[0], 8),
        "conf_lnb_col": _col(conf_ln_b[0], 8),
        "short_w_col": sw_col,
        "ident": np.eye(128, dtype=np.float32),
    }


def kernel(x, meta, g_pre_mix, w_in, b_gates, conf_dw_w, conf_dw_b, conf_ln_g, conf_ln_b, conf_w_pw,
           short_dw_w, short_w_out, w_o, g_post_mix, g_pre_mlp, w_up, w_down, g_post_mlp):
    if "nc" not in _NC_CACHE:
        _NC_CACHE["nc"] = build_nc()
    nc = _NC_CACHE["nc"]
    x = np.ascontiguousarray(np.asarray(x, dtype=np.float32))
    shared = make_shared(meta, g_pre_mix, w_in, b_gates, conf_dw_w, conf_dw_b, conf_ln_g, conf_ln_b, conf_w_pw,
                         short_dw_w, short_w_out, w_o, g_post_mix, g_pre_mlp, w_up, w_down, g_post_mlp)
    in_maps = []
    for b in range(N_CORES):
        m = dict(shared)
        m["x"] = x[b]
        in_maps.append(m)
    res = run_bass_kernel_spmd(nc, in_maps, core_ids=list(range(N_CORES)))
    return np.stack([np.asarray(r["out"], dtype=np.float32) for r in res.results], axis=0)
```

```python
import itertools
from contextlib import ExitStack

import numpy as np
import concourse.bass as bass
import concourse.mybir as mybir
from concourse.bass_utils import run_bass_kernel_spmd

F32 = mybir.dt.float32
BF16 = mybir.dt.bfloat16
AF = mybir.ActivationFunctionType
ALU = mybir.AluOpType

D = 2048
SEQ = 2048
NMETA = 16
DC = 1024
DFF = 8192
INC = 9216
TT = 512
NT = SEQ // TT
RMS_EPS = 1e-6
LN_EPS = 1e-5
KIB = 1024
N_CORES = 8


class V:
    __slots__ = ("ap", "blk")

    def __init__(self, ap, blk):
        self.ap = ap
        self.blk = blk


def _ranges(off, dims):
    dims = [(s, c) for s, c in dims if c > 1 and s != 0]
    if not dims:
        return [(off, off + 1)]
    if dims[-1][0] == 1:
        inner = dims[-1][1]
        outer = dims[:-1]
    else:
        inner = 1
        outer = dims
    n = 1
    for _, c in outer:
        n *= c
    if n > 512:
        hi = off + sum(s * (c - 1) for s, c in dims) + 1
        return [(off, hi)]
    out = []
    for idx in itertools.product(*[range(c) for _, c in outer]):
        o = off + sum(i * s for i, (s, _) in zip(idx, outer))
        out.append((o, o + inner))
    return out


class Buf:
    def __init__(self, nc, name, shape, dtype, off):
        self.t = nc.alloc_sbuf_tensor_at(name, list(shape), dtype, offset=off)
        self.ap = self.t.ap()
        self.off = off
        self.esz = 4 if dtype == F32 else 2
        self.pstep = int(np.prod(shape[1:]))
        self.nbytes = self.pstep * self.esz

    def __getitem__(self, idx):
        v = self.ap[idx]
        fo = v.offset % self.pstep
        dims = list(v.ap)[1:]
        blk = []
        for lo, hi in _ranges(fo, dims):
            b0 = (self.off + lo * self.esz) // 64
            b1 = (self.off + hi * self.esz - 1) // 64
            blk.extend(("s", b) for b in range(b0, b1 + 1))
        return V(v, blk)


class Sched:
    ENG = ("pe", "act", "dve", "pool", "sp")

    def __init__(self):
        self.ops = {e: [] for e in self.ENG}
        self.cnt = {}
        self.waited = {}
        self.lastw = {}
        self.readers = {}

    @staticmethod
    def _blks(items):
        for it in items:
            if isinstance(it, V):
                yield from it.blk
            else:
                yield it

    def _deps(self, eng, reads, writes, selfsync):
        deps = {}

        def add(tok):
            if tok is None:
                return
            k, v = tok
            if k == eng and not selfsync:
                return
            if deps.get(k, 0) < v:
                deps[k] = v

        for b in reads:
            add(self.lastw.get(b))
            if b[0] == "p":
                r = self.readers.get(b)
                if r:
                    for k, v in r.items():
                        add((k, v))
        for b in writes:
            add(self.lastw.get(b))
            r = self.readers.get(b)
            if r:
                for k, v in r.items():
                    add((k, v))
        waits = []
        for k, v in deps.items():
            if self.waited.get((eng, k), 0) < v:
                self.waited[(eng, k)] = v
                waits.append((k, v))
        return waits

    def _commit(self, tok, reads, writes):
        k, v = tok
        for b in reads:
            r = self.readers.setdefault(b, {})
            if r.get(k, 0) < v:
                r[k] = v
        for b in writes:
            self.lastw[b] = tok
            self.readers[b] = {}

    def op(self, eng, fns, reads, writes, selfsync=False):
        if callable(fns):
            fns = [fns]
        reads = list(self._blks(reads))
        writes = list(self._blks(writes))
        waits = self._deps(eng, reads, writes, selfsync)
        self.cnt[eng] = self.cnt.get(eng, 0) + 1
        tok = (eng, self.cnt[eng])
        self._commit(tok, reads, writes)
        self.ops[eng].append((waits, fns, eng, 1, False))
        return tok

    def dma(self, eng, semkey, fns, reads, writes):
        if callable(fns):
            fns = [fns]
        reads = list(self._blks(reads))
        writes = list(self._blks(writes))
        waits = self._deps(eng, reads, writes, True)
        self.cnt[semkey] = self.cnt.get(semkey, 0) + 16 * len(fns)
        tok = (semkey, self.cnt[semkey])
        self._commit(tok, reads, writes)
        self.ops[eng].append((waits, fns, semkey, 16, True))
        return tok


class _Stop(Exception):
    pass


def build_nc(stop=None, conv_window=5):
    nc = bass.Bass("TRN2", target_bir_lowering=False)
    dumps = []

    def checkpoint(name, views):
        if stop == name:
            dumps.extend(views)
            raise _Stop()

    def din(name, shape):
        return nc.dram_tensor(name, list(shape), F32, kind="ExternalInput").ap()

    x_d = din("x", [SEQ, D])
    meta_d = din("meta", [NMETA, D])
    w_in_d = din("w_in", [D, INC])
    pw_d = din("conf_w_pw", [DC, D])
    so_d = din("short_w_out", [DC, D])
    wo_d = din("w_o", [D, D])
    up_d = din("w_up", [D, DFF])
    dn_d = din("w_down", [DFF, D])
    gpm_d = din("gcol_pre_mix", [128, 16])
    gpl_d = din("gcol_pre_mlp", [128, 16])
    gpost_mix_d = din("g_post_mix", [1, D])
    gpost_mlp_d = din("g_post_mlp", [1, D])
    bgc_d = din("bgates_col", [128, 32])
    cw_d = din("conf_w_col", [128, 8 * 31])
    cb_d = din("conf_b_col", [128, 8])
    lng_d = din("conf_lng_col", [128, 8])
    lnb_d = din("conf_lnb_col", [128, 8])
    sw_d = din("short_w_col", [128, 8 * 3])
    ident_d = din("ident", [128, 128])
    out_d = nc.dram_tensor("out", [SEQ, D], F32, kind="ExternalOutput").ap()

    s_win = nc.dram_tensor("s_win", [D, INC], BF16, kind="Internal").ap()
    s_pw = nc.dram_tensor("s_pw", [DC, D], BF16, kind="Internal").ap()
    s_so = nc.dram_tensor("s_so", [DC, D], BF16, kind="Internal").ap()
    s_wo = nc.dram_tensor("s_wo", [D, D], BF16, kind="Internal").ap()
    s_up = nc.dram_tensor("s_up", [D, DFF], BF16, kind="Internal").ap()
    s_dn = nc.dram_tensor("s_dn", [DFF, D], BF16, kind="Internal").ap()

    def B(name, shape, dtype, off):
        return Buf(nc, name, shape, dtype, off)

    wbuf = [B("wbuf0", [128, 16, 512], BF16, 16 * KIB), B("wbuf1", [128, 16, 512], BF16, 32 * KIB)]
    h = B("h", [128, 4, 2048], F32, 48 * KIB)
    gbc = B("gbc", [128, 2048], F32, 80 * KIB)
    cbuf = B("c", [128, 8, 512], F32, 88 * KIB)
    nT = B("nT", [128, 16, 512], BF16, 104 * KIB)
    park3 = B("park3", [128, 4, 2048], F32, 88 * KIB)
    R2 = 120 * KIB
    fT = B("fT", [128, 64, 512], BF16, R2)
    mT = B("mT", [128, 16, 512], BF16, R2)
    o = R2 + 16 * KIB
    a_bf = B("a_bf", [128, 8, 544], BF16, o)
    o += 8 * 544 * 2
    cv_bf = B("cv_bf", [128, 8, 520], BF16, o)
    o += 8 * 520 * 2
    aln = B("aln", [128, 8, 512], BF16, o)
    o += 8192
    s_bf = B("s_bf", [128, 8, 512], BF16, o)
    o += 8192
    park1 = B("park1", [128, 4, 2048], F32, R2 + 16 * KIB)
    assert o % 64 == 0 and o >= R2 + 48 * KIB
    parkB = B("parkB", [128, 4, 512], F32, o)
    o += 8192
    cbf_t = [B("cbf0", [128, 512], BF16, o), B("cbf1", [128, 512], BF16, o + 1024)]
    csq_t = [B("csq0", [128, 512], BF16, o + 2048), B("csq1", [128, 512], BF16, o + 3072)]
    o += 4096
    assert o <= R2 + 64 * KIB
    xs = B("xs", [128, 2048], F32, 184 * KIB)
    parkA = B("parkA", [128, 4, 512], F32, 184 * KIB)
    xs2 = B("xs2", [128, 2048], F32, parkB.off)
    xs_bufs = [xs, xs2]
    dbuf = [B("dbuf0", [128, 31, 128], BF16, 192 * KIB), B("dbuf1", [128, 31, 128], BF16, 192 * KIB + 7936)]
    o = 192 * KIB + 2 * 7936
    assert o % 64 == 0

    def small(name, shape, dtype):
        nonlocal o
        b = B(name, shape, dtype, o)
        o += (b.nbytes + 63) // 64 * 64
        return b

    lnv = small("lnv", [128, 512], F32)
    junk = B("junk", [128, 1024], BF16, lnv.off)
    tmpA = [small("tmpA0", [128, 512], F32), small("tmpA1", [128, 512], F32)]
    tmpB = [small("tmpB0", [128, 512], F32), small("tmpB1", [128, 512], F32)]
    ident = small("ident", [128, 128], F32)
    ones_bf = small("ones_bf", [128, 128], BF16)
    gpm = small("gpm", [128, 16], F32)
    gpl = small("gpl", [128, 16], F32)
    bgc = small("bgc", [128, 32], F32)
    cwc = small("cwc", [128, 8, 31], F32)
    cbc = small("cbc", [128, 8], F32)
    lngc = small("lngc", [128, 8], F32)
    lnbc = small("lnbc", [128, 8], F32)
    swc = small("swc", [128, 8, 3], F32)
    hist_a = small("hist_a", [128, 8, 32], BF16)
    hist_cv = small("hist_cv", [128, 8, 2], BF16)
    d3buf = [small("d3b0", [128, 3, 128], BF16), small("d3b1", [128, 3, 128], BF16)]
    neghalf = small("neghalf", [128, 1], F32)
    nTm = small("nTm", [128, 16, NMETA], BF16)
    parkAm = small("parkAm", [128, 4, NMETA], F32)
    st = small("st", [128, 64], F32)
    assert o <= 224 * KIB, o

    pst = nc.alloc_psum_tensor("ps", [128, 8, 512], F32).ap()

    def PS(b, lo=0, hi=512, np_=128):
        return V(pst[0:np_, b, lo:hi], [("p", b)])

    S = Sched()
    bank_ctr = [0]

    def alloc_bank():
        b = bank_ctr[0] % 8
        bank_ctr[0] += 1
        return b

    def ACT(fn, r, w, **kw):
        return S.op("act", fn, r, w, **kw)

    def DVE(fn, r, w, **kw):
        return S.op("dve", fn, r, w, **kw)

    def POOL(fn, r, w, **kw):
        return S.op("dve", fn, r, w, **kw)

    def PE(fns, r, w, **kw):
        return S.op("pe", fns, r, w, **kw)

    C_SS = 0
    C_MS = 8
    C_RSTD = 12
    C_SSQ = 16
    C_MS2 = 32
    C_RSTD2 = 36

    cl = []
    for dst, src in ((ident, ident_d), (gpm, gpm_d), (gpl, gpl_d), (bgc, bgc_d), (cbc, cb_d),
                     (lngc, lng_d), (lnbc, lnb_d)):
        cl.append((dst[:, :], src))
    cl.append((cwc[:, :, :], cw_d.rearrange("p (j k) -> p j k", k=31)))
    cl.append((swc[:, :, :], sw_d.rearrange("p (j k) -> p j k", k=3)))
    S.dma("pool", "const", [(lambda e, d=d, s=s: e.dma_start(out=d.ap, in_=s)) for d, s in cl],
          [], [d for d, _ in cl])

    conv_id = [0]
    conv_keys = []
    pending = []

    def convert(pieces, key):
        pending.append((pieces, key))

    conv_desc = []

    def issue_conv(n):
        budget = 128 * conv_window
        for _ in range(n):
            if not pending:
                return
            pieces, key = pending.pop(0)
            i = conv_id[0]
            conv_id[0] += 1
            nd = sum(int(d.shape[0]) for d, _ in pieces) // 16
            conv_desc.append(nd)
            tot = nd
            j = i - 1
            while j >= 0 and tot + conv_desc[j] <= budget:
                tot += conv_desc[j]
                j -= 1
            thr = [conv_keys[j]] if j >= 0 else []
            conv_keys.append(key)
            S.dma("pool", f"cv{i}", [(lambda e, d=d, s=s: e.dma_start(out=d, in_=s)) for d, s in pieces], thr, [key])

    POOL(lambda e: e.memset(gbc.ap, 0.0), [], [gbc[:, :]])
    S.dma("act", "gbc", lambda e: e.dma_start(out=gbc[0:NMETA, :].ap, in_=meta_d), [], [gbc[:, :]])

    def halves(dst, src_):
        dv = dst.rearrange("(r two) c -> two r c", two=2)
        sv = src_.rearrange("(r two) c -> two r c", two=2)
        return [(dv[0], sv[0]), (dv[1], sv[1])]

    for cb in range(5):
        c0, c1 = cb * 2048, min((cb + 1) * 2048, INC)
        convert([(s_win[:, c0:c1], w_in_d[:, c0:c1])], ("d", "win", cb))
    convert(halves(s_pw, pw_d), ("d", "pw"))
    convert(halves(s_so, so_d), ("d", "so"))
    convert(halves(s_wo, wo_d), ("d", "wo"))
    for cb in range(4):
        convert([(s_up[:, cb * 2048:(cb + 1) * 2048], up_d[:, cb * 2048:(cb + 1) * 2048])], ("d", "up", cb))
    for rb in range(4):
        convert(halves(s_dn[rb * 2048:(rb + 1) * 2048, :], dn_d[rb * 2048:(rb + 1) * 2048, :]), ("d", "dn", rb))

    issue_conv(1000)
    POOL(lambda e: e.memset(neghalf.ap, -0.5), [], [neghalf[:, :]])
    POOL(lambda e: e.memset(ones_bf.ap, 1.0 / DC), [], [ones_bf[:, :]])
    POOL(lambda e: e.memset(hist_a.ap, 0.0), [], [hist_a[:, :, :]])
    POOL(lambda e: e.memset(hist_cv.ap, 0.0), [], [hist_cv[:, :, :]])

    slot_ctr = [0]
    win_v = s_win.rearrange("(k p) c -> p k c", p=128)
    pw_v = s_pw.rearrange("(k p) c -> p k c", p=128)
    so_v = s_so.rearrange("(k p) c -> p k c", p=128)
    wo_v = s_wo.rearrange("(k p) c -> p k c", p=128)
    up_v = s_up.rearrange("(k p) c -> p k c", p=128)
    dn_v = s_dn.rearrange("(k p) c -> p k c", p=128)

    def load_unit(parts, keys):
        slot = slot_ctr[0] % 2
        slot_ctr[0] += 1
        wb = wbuf[slot]
        fns = [(lambda e, k0=k0, k1=k1, src=src: e.dma_start(out=wb.ap[:, k0:k1, :], in_=src))
               for k0, k1, src in parts]
        S.dma("sp", f"w{slot}", fns, keys, [wb[:, :, :]])
        return slot

    def load_win(col0):
        return load_unit([(0, 16, win_v[:, :, col0:col0 + 512])], [("d", "win", col0 // 2048)])

    def rstd_from(np_, ss_cols, ms_col, rstd_col, eps, denom):
        c0, n = ss_cols
        ms = st[0:np_, ms_col:ms_col + 1]
        DVE(lambda e: e.tensor_reduce(out=ms.ap, in_=st.ap[0:np_, c0:c0 + n],
                                      axis=mybir.AxisListType.X, op=ALU.add),
            [st[0:np_, c0:c0 + n]], [ms])
        DVE(lambda e: e.tensor_scalar(out=ms.ap, in0=ms.ap, scalar1=1.0 / denom, scalar2=eps,
                                      op0=ALU.mult, op1=ALU.add), [ms], [ms], selfsync=True)
        rs = st[0:np_, rstd_col:rstd_col + 1]
        ACT(lambda e: e.activation(out=ms.ap, in_=ms.ap, func=AF.Sqrt), [ms], [ms])
        DVE(lambda e: e.reciprocal(out=rs.ap, in_=ms.ap), [ms], [rs])
        return rs

    def norm_T(hrow, np_, s, gcol, col0, dstbuf=None, nout=None):
        dstbuf = nT if dstbuf is None else dstbuf
        nout = np_ if nout is None else nout
        for hh in range(2):
            cc = C_SS + s * 2 + hh
            ACT(lambda e, hh=hh, cc=cc: e.activation(out=junk.ap[0:np_, :], in_=hrow.ap[:, hh * 1024:(hh + 1) * 1024],
                                                     func=AF.Square, accum_out=st.ap[0:np_, cc:cc + 1]),
                [hrow], [junk[0:np_, :], st[0:np_, cc:cc + 1]])
        checkpoint("nt1", [("st", st[:, :])])
        rs = rstd_from(np_, (C_SS + s * 2, 2), C_MS + s, C_RSTD + s, RMS_EPS, float(D))
        checkpoint("nt2", [("st", st[:, :])])
        xsb = xs_bufs[s % 2]
        xv = xsb[0:np_, :]
        ACT(lambda e: e.activation(out=xv.ap, in_=hrow.ap, func=AF.Identity, scale=rs.ap), [hrow, rs], [xv])
        checkpoint("nt3", [("xs", xsb[:, :])])
        for kb in range(4):
            b = alloc_bank()
            fns = [(lambda e, k=kb * 4 + i, i=i, b=b: e.transpose(out=pst[:, b, i * 128:i * 128 + np_],
                                                            in_=xsb.ap[0:np_, k * 128:(k + 1) * 128],
                                                            identity=ident.ap[0:np_, 0:np_]))
                   for i in range(4)]
            PE(fns, [xv, ident[:, :]], [("p", b)])
            checkpoint("nt4", [("st", st[:, :])])
            for i in range(4):
                k = kb * 4 + i
                dst = dstbuf[:, k, col0:col0 + nout]
                src = PS(b, i * 128, i * 128 + nout)
                if kb % 2 == 0:
                    ACT(lambda e, dst=dst, src=src, k=k: e.activation(out=dst.ap, in_=src.ap, func=AF.Identity,
                                                                      scale=gcol.ap[:, k:k + 1]),
                        [src, gcol[:, k:k + 1]], [dst])
                    checkpoint("nt5", [("st", st[:, :])])
                else:
                    DVE(lambda e, dst=dst, src=src, k=k: e.tensor_scalar(out=dst.ap, in0=src.ap,
                                                                         scalar1=gcol.ap[:, k:k + 1], scalar2=None,
                                                                         op0=ALU.mult),
                        [src, gcol[:, k:k + 1]], [dst])

    def formF_chunk(slot, k0, kn, jj, rhs_buf, rk0, ncols):
        b = alloc_bank()
        wb = wbuf[slot]
        fns = [(lambda e, k=k: e.matmul(out=pst[:, b, 0:ncols], lhsT=wb.ap[:, k0 + k, jj * 128:(jj + 1) * 128],
                                        rhs=rhs_buf.ap[:, rk0 + k, 0:ncols], start=(k == 0), stop=(k == kn - 1)))
               for k in range(kn)]
        PE(fns, [wb[:, k0:k0 + kn, jj * 128:(jj + 1) * 128], rhs_buf[:, rk0:rk0 + kn, 0:ncols]], [("p", b)])
        return b

    def build_d3(j):
        d3 = d3buf[j % 2]
        POOL(lambda e: e.tensor_tensor(out=d3.ap, in0=ident.ap.unsqueeze(1).broadcast_to([128, 3, 128]),
                                       in1=swc.ap[:, j, :].unsqueeze(2).broadcast_to([128, 3, 128]), op=ALU.mult),
             [ident[:, :], swc[:, j, :]], [d3[:, :, :]])

    def build_d31(j):
        dd = dbuf[j % 2]
        POOL(lambda e: e.tensor_tensor(out=dd.ap, in0=ident.ap.unsqueeze(1).broadcast_to([128, 31, 128]),
                                       in1=cwc.ap[:, j, :].unsqueeze(2).broadcast_to([128, 31, 128]), op=ALU.mult),
             [ident[:, :], cwc[:, j, :]], [dd[:, :, :]])

    def win_part1(with_meta):
        def meta_chunk(slot, jj):
            b = formF_chunk(slot, 0, 16, jj, nTm, 0, NMETA)
            return PS(b, 0, NMETA)

        for half in range(2):
            slot = load_win(1024 + half * 512)
            for jj in range(4):
                b = formF_chunk(slot, 0, 16, jj, nT, 0, TT)
                src = PS(b)
                dst = parkA[:, jj, :]
                ACT(lambda e, dst=dst, src=src: e.activation(out=dst.ap, in_=src.ap, func=AF.Sigmoid), [src], [dst])
                if with_meta:
                    src = meta_chunk(slot, jj)
                    dst = parkAm[:, jj, :]
                    ACT(lambda e, dst=dst, src=src: e.activation(out=dst.ap, in_=src.ap, func=AF.Sigmoid), [src], [dst])
            slot = load_win(0 + half * 512)
            for jj in range(4):
                j = half * 4 + jj
                b = formF_chunk(slot, 0, 16, jj, nT, 0, TT)
                src = PS(b)
                pk = parkA[:, jj, :]
                dst = a_bf[:, j, 30:542]
                DVE(lambda e, dst=dst, src=src, pk=pk: e.tensor_tensor(out=dst.ap, in0=src.ap, in1=pk.ap, op=ALU.mult),
                    [src, pk], [dst])
                if with_meta:
                    src = meta_chunk(slot, jj)
                    pk = parkAm[:, jj, :]
                    dst = a_bf[:, j, 14:30]
                    DVE(lambda e, dst=dst, src=src, pk=pk: e.tensor_tensor(out=dst.ap, in0=src.ap, in1=pk.ap,
                                                                           op=ALU.mult), [src, pk], [dst])
        for half in range(2):
            slot = load_win(2048 + half * 512)
            for jj in range(4):
                b = formF_chunk(slot, 0, 16, jj, nT, 0, TT)
                src = PS(b)
                dst = parkB[:, jj, :]
                ACT(lambda e, dst=dst, src=src: e.activation(out=dst.ap, in_=src.ap, func=AF.Copy), [src], [dst])
            slot = load_win(3072 + half * 512)
            for jj in range(4):
                b = formF_chunk(slot, 0, 16, jj, nT, 0, TT)
                src = PS(b)
                dst = parkA[:, jj, :]
                ACT(lambda e, dst=dst, src=src: e.activation(out=dst.ap, in_=src.ap, func=AF.Copy), [src], [dst])
                if with_meta:
                    src = meta_chunk(slot, jj)
                    dst = parkAm[:, jj, :]
                    ACT(lambda e, dst=dst, src=src: e.activation(out=dst.ap, in_=src.ap, func=AF.Copy), [src], [dst])
            slot = load_win(4096 + half * 512)
            for jj in range(4):
                j = half * 4 + jj
                b = formF_chunk(slot, 0, 16, jj, nT, 0, TT)
                src = PS(b)
                pk = parkA[:, jj, :]
                dst = cv_bf[:, j, 2:514]
                DVE(lambda e, dst=dst, src=src, pk=pk: e.tensor_tensor(out=dst.ap, in0=src.ap, in1=pk.ap, op=ALU.mult),
                    [src, pk], [dst])
                if with_meta:
                    b = formF_chunk(slot, 0, 16, jj, nTm, 0, NMETA)
                    src = PS(b, 14, 16)
                    pk = parkAm[:, jj, 14:16]
                    dst = cv_bf[:, j, 0:2]
                    DVE(lambda e, dst=dst, src=src, pk=pk: e.tensor_tensor(out=dst.ap, in0=src.ap, in1=pk.ap,
                                                                           op=ALU.mult), [src, pk], [dst])
            for jj in range(4):
                j = half * 4 + jj
                build_d3(j)
                d3 = d3buf[j % 2]
                b = alloc_bank()
                fns = [(lambda e, k=k, j=j, d3=d3, b=b: e.matmul(out=pst[:, b, :], lhsT=d3.ap[:, k, :],
                                                             rhs=cv_bf.ap[:, j, k:k + 512],
                                                             start=(k == 0), stop=(k == 2))) for k in range(3)]
                PE(fns, [d3[:, :, :], cv_bf[:, j, 0:514]], [("p", b)])
                src = PS(b)
                pk = parkB[:, jj, :]
                dst = s_bf[:, j, :]
                DVE(lambda e, dst=dst, src=src, pk=pk: e.tensor_tensor(out=dst.ap, in0=src.ap, in1=pk.ap,
                                                                       op=ALU.mult), [src, pk], [dst])

    def prologue():
        checkpoint("init", [("cwc", cwc[:, :, :]), ("ones", ones_bf[:, :])])
        norm_T(gbc[:, :], 128, 0, gpm, 0, dstbuf=nTm, nout=NMETA)
        checkpoint("pro_norm", [("nT", nTm[:, :, :]), ("st", st[:, :])])

    def tile(t):
        for s in range(4):
            r0 = t * TT + s * 128
            S.dma("act", f"x{s}", lambda e, s=s, r0=r0: e.dma_start(out=h.ap[:, s, :], in_=x_d[r0:r0 + 128, :]),
                  [], [h[:, s, :]])
        if t == 0:
            POOL(lambda e: e.memset(a_bf.ap[:, :, 0:14], 0.0), [], [a_bf[:, :, 0:14]])
        else:
            POOL(lambda e: e.tensor_copy(out=a_bf.ap[:, :, 0:30], in_=hist_a.ap[:, :, 0:30]),
                 [hist_a[:, :, 0:30]], [a_bf[:, :, 0:30]])
            POOL(lambda e: e.tensor_copy(out=cv_bf.ap[:, :, 0:2], in_=hist_cv.ap[:, :, 0:2]),
                 [hist_cv[:, :, :]], [cv_bf[:, :, 0:2]])
        for s in range(4):
            norm_T(h[:, s, :], 128, s, gpm, s * 128)

        issue_conv(2)
        if t == 0:
            checkpoint("t0_norm", [("nT", nT[:, :, :])])
        win_part1(t == 0)
        issue_conv(2)
        if t == 0:
            checkpoint("t0_win1", [("a_bf", a_bf[:, :, :]), ("cv_bf", cv_bf[:, :, :]), ("s_bf", s_bf[:, :, :])])

        POOL(lambda e: e.tensor_copy(out=hist_a.ap[:, :, 0:30], in_=a_bf.ap[:, :, 512:542]),
             [a_bf[:, :, 512:542]], [hist_a[:, :, 0:30]], selfsync=True)
        POOL(lambda e: e.tensor_copy(out=hist_cv.ap[:, :, 0:2], in_=cv_bf.ap[:, :, 512:514]),
             [cv_bf[:, :, 512:514]], [hist_cv[:, :, :]], selfsync=True)

        S1 = alloc_bank()
        S2 = alloc_bank()
        build_d31(0)
        build_d31(1)

        def stats_mm(j):
            cb_, cs_ = cbf_t[j % 2], csq_t[j % 2]
            PE(lambda e: e.matmul(out=pst[:, S1, :], lhsT=ones_bf.ap, rhs=cb_.ap, start=(j == 0), stop=(j == 7)),
               [ones_bf[:, :], cb_[:, :]], [("p", S1)])
            PE(lambda e: e.matmul(out=pst[:, S2, :], lhsT=ones_bf.ap, rhs=cs_.ap, start=(j == 0), stop=(j == 7)),
               [ones_bf[:, :], cs_[:, :]], [("p", S2)])

        for j in range(8):
            dd = dbuf[j % 2]
            b = alloc_bank()
            while b in (S1, S2):
                b = alloc_bank()
            fns = [(lambda e, k=k, j=j, dd=dd, b=b: e.matmul(out=pst[:, b, :], lhsT=dd.ap[:, k, :],
                                                             rhs=a_bf.ap[:, j, k:k + 512],
                                                             start=(k == 0), stop=(k == 30))) for k in range(31)]
            PE(fns, [dd[:, :, :], a_bf[:, j, 0:542]], [("p", b)])
            if j + 2 < 8:
                build_d31(j + 2)
            src = PS(b)
            cbj = cbc[:, j:j + 1]
            ACT(lambda e, src=src, j=j, cbj=cbj: e.activation(out=cbuf.ap[:, j, :], in_=src.ap, func=AF.Identity,
                                                              bias=cbj.ap), [src, cbj], [cbuf[:, j, :]])
            cb_, cs_ = cbf_t[j % 2], csq_t[j % 2]
            DVE(lambda e, j=j, cb_=cb_: e.tensor_copy(out=cb_.ap, in_=cbuf.ap[:, j, :]),
                [cbuf[:, j, :]], [cb_[:, :]])
            ACT(lambda e, src=src, cbj=cbj, cs_=cs_: e.activation(out=cs_.ap, in_=src.ap, func=AF.Square,
                                                                  bias=cbj.ap), [src, cbj], [cs_[:, :]])
            if j >= 1:
                stats_mm(j - 1)
        stats_mm(7)

        s1v, s2v = PS(S1), PS(S2)
        ACT(lambda e: e.activation(out=lnv.ap, in_=s1v.ap, func=AF.Square), [s1v], [lnv[:, :]])
        DVE(lambda e: e.tensor_tensor(out=lnv.ap, in0=s2v.ap, in1=lnv.ap, op=ALU.subtract),
            [s2v, lnv[:, :]], [lnv[:, :]])
        DVE(lambda e: e.tensor_scalar(out=lnv.ap, in0=lnv.ap, scalar1=LN_EPS, scalar2=0.0, op0=ALU.add, op1=ALU.max),
            [lnv[:, :]], [lnv[:, :]], selfsync=True)
        ACT(lambda e: e.activation(out=lnv.ap, in_=lnv.ap, func=AF.Sqrt), [lnv[:, :]], [lnv[:, :]])
        DVE(lambda e: e.reciprocal(out=lnv.ap, in_=lnv.ap), [lnv[:, :]], [lnv[:, :]])

        def ln_apply(j):
            ta, tb = tmpA[j % 2], tmpB[j % 2]
            DVE(lambda e: e.tensor_tensor(out=ta.ap, in0=cbuf.ap[:, j, :], in1=s1v.ap, op=ALU.subtract),
                [cbuf[:, j, :], s1v], [ta[:, :]])
            DVE(lambda e: e.tensor_tensor(out=tb.ap, in0=ta.ap, in1=lnv.ap, op=ALU.mult),
                [ta[:, :], lnv[:, :]], [tb[:, :]], selfsync=True)
            ACT(lambda e: e.activation(out=tb.ap, in_=tb.ap, func=AF.Identity, scale=lngc.ap[:, j:j + 1],
                                       bias=lnbc.ap[:, j:j + 1]),
                [tb[:, :], lngc[:, j:j + 1], lnbc[:, j:j + 1]], [tb[:, :]])
            ACT(lambda e: e.activation(out=ta.ap, in_=tb.ap, func=AF.Sigmoid), [tb[:, :]], [ta[:, :]], selfsync=True)
            POOL(lambda e: e.tensor_tensor(out=aln.ap[:, j, :], in0=tb.ap, in1=ta.ap, op=ALU.mult),
                 [ta[:, :], tb[:, :]], [aln[:, j, :]])

        for j in range(8):
            ln_apply(j)

        issue_conv(2)
        if t == 0:
            checkpoint("t0_conv", [("c", cbuf[:, :, :]), ("aln", aln[:, :, :]), ("lnv", lnv[:, :])])
        for q in range(4):
            slot = load_win(5120 + q * 512)
            for jj in range(4):
                b = formF_chunk(slot, 0, 16, jj, nT, 0, TT)
                src = PS(b)
                col = bgc[:, q * 4 + jj:q * 4 + jj + 1]
                dst = parkA[:, jj, :]
                ACT(lambda e, dst=dst, src=src, col=col: e.activation(out=dst.ap, in_=src.ap, func=AF.Sigmoid,
                                                                      bias=col.ap), [src, col], [dst])
            slot = load_win(7168 + q * 512)
            for jj in range(4):
                b = formF_chunk(slot, 0, 16, jj, nT, 0, TT)
                src = PS(b)
                col = bgc[:, 16 + q * 4 + jj:16 + q * 4 + jj + 1]
                dst = parkB[:, jj, :]
                ACT(lambda e, dst=dst, src=src, col=col: e.activation(out=dst.ap, in_=src.ap, func=AF.Sigmoid,
                                                                      bias=col.ap), [src, col], [dst])
            slot = load_unit([(0, 8, pw_v[:, :, q * 512:(q + 1) * 512]), (8, 16, so_v[:, :, q * 512:(q + 1) * 512])],
                             [("d", "pw"), ("d", "so")])
            for jj in range(4):
                b = formF_chunk(slot, 0, 8, jj, aln, 0, TT)
                src = PS(b)
                pk = parkA[:, jj, :]
                DVE(lambda e, src=src, pk=pk: e.tensor_tensor(out=pk.ap, in0=src.ap, in1=pk.ap, op=ALU.mult),
                    [src, pk], [pk])
            for jj in range(4):
                b = formF_chunk(slot, 8, 8, jj, s_bf, 0, TT)
                src = PS(b)
                pk = parkB[:, jj, :]
                DVE(lambda e, src=src, pk=pk: e.tensor_tensor(out=pk.ap, in0=src.ap, in1=pk.ap, op=ALU.mult),
                    [src, pk], [pk])
                pa = parkA[:, jj, :]
                dst = mT[:, q * 4 + jj, :]
                POOL(lambda e, dst=dst, pa=pa, pk=pk: e.tensor_tensor(out=dst.ap, in0=pa.ap, in1=pk.ap, op=ALU.add),
                     [pa, pk], [dst], selfsync=True)

        issue_conv(2)
        if t == 0:
            checkpoint("t0_m", [("mT", mT[:, :, :])])
        def formT_stage(act_buf, nk_groups, wview, key_fn, gpost_d, park):
            S.dma("sp", "gbc", lambda e: e.dma_start(out=gbc.ap, in_=gpost_d.broadcast_to([128, D])),
                  [], [gbc[:, :]])
            for j in range(4):
                banks = [alloc_bank() for _ in range(4)]
                for kg in range(nk_groups):
                    slot = load_unit([(0, 16, wview[:, kg * 16:(kg + 1) * 16, j * 512:(j + 1) * 512])], [key_fn(kg)])
                    wb = wbuf[slot]
                    fns = []
                    for k in range(16):
                        for s in range(4):
                            fns.append(lambda e, k=k, s=s, kg=kg, wb=wb, banks=tuple(banks): e.matmul(
                                out=pst[:, banks[s], :], lhsT=act_buf.ap[:, kg * 16 + k, s * 128:(s + 1) * 128],
                                rhs=wb.ap[:, k, :], start=(kg == 0 and k == 0),
                                stop=(kg == nk_groups - 1 and k == 15)))
                    PE(fns, [wb[:, :, :], act_buf[:, kg * 16:(kg + 1) * 16, :]], [("p", bb) for bb in banks])
                for s in range(4):
                    src = PS(banks[s])
                    cc = C_SSQ + s * 4 + j
                    ACT(lambda e, src=src, cc=cc: e.activation(out=junk.ap[:, 0:512], in_=src.ap, func=AF.Square,
                                                               accum_out=st.ap[:, cc:cc + 1]),
                        [src], [junk[:, 0:512], st[:, cc:cc + 1]])
                    dst = park[:, s, j * 512:(j + 1) * 512]
                    gs = gbc[:, j * 512:(j + 1) * 512]
                    DVE(lambda e, dst=dst, src=src, gs=gs: e.tensor_tensor(out=dst.ap, in0=src.ap, in1=gs.ap,
                                                                           op=ALU.mult), [src, gs], [dst])
            for s in range(4):
                rs = rstd_from(128, (C_SSQ + s * 4, 4), C_MS2 + s, C_RSTD2 + s, RMS_EPS, float(D))
                hv = h[:, s, :]
                pv = park[:, s, :]
                DVE(lambda e, hv=hv, pv=pv, rs=rs: e.scalar_tensor_tensor(out=hv.ap, in0=pv.ap, scalar=rs.ap,
                                                                          in1=hv.ap, op0=ALU.mult, op1=ALU.add),
                    [pv, rs, hv], [hv], selfsync=True)

        formT_stage(mT, 1, wo_v, lambda kg: ("d", "wo"), gpost_mix_d, park1)
        issue_conv(2)
        if t == 0:
            checkpoint("t0_wo", [("h", h[:, :, :])])

        for s in range(4):
            norm_T(h[:, s, :], 128, s, gpl, s * 128)
        issue_conv(4)
        if t == 0:
            checkpoint("t0_n2", [("nT", nT[:, :, :])])
        for u in range(16):
            slot = load_unit([(0, 16, up_v[:, :, u * 512:(u + 1) * 512])], [("d", "up", u // 4)])
            for jj in range(4):
                ff = u * 4 + jj
                b = formF_chunk(slot, 0, 16, jj, nT, 0, TT)
                src = PS(b)
                xf = tmpA[ff % 2]
                ACT(lambda e, src=src, xf=xf: e.activation(out=xf.ap, in_=src.ap, func=AF.Copy), [src], [xf[:, :]])
                dst = fT[:, ff, :]
                DVE(lambda e, dst=dst, src=src, xf=xf: e.scalar_tensor_tensor(out=dst.ap, in0=src.ap, scalar=0.0,
                                                                              in1=xf.ap, op0=ALU.max, op1=ALU.mult),
                    [src, xf[:, :]], [dst])
        if t == 0:
            checkpoint("t0_up", [("fT", fT[:, :, :])])
        formT_stage(fT, 4, dn_v, lambda kg: ("d", "dn", kg), gpost_mlp_d, park3)
        if t == 0:
            checkpoint("t0_end", [("h", h[:, :, :]), ("st", st[:, :]), ("park3", park3[:, :, :])])

        for s in range(4):
            r0 = t * TT + s * 128
            S.dma("pool", f"o{s}", lambda e, s=s, r0=r0: e.dma_start(out=out_d[r0:r0 + 128, :], in_=h.ap[:, s, :]),
                  [h[:, s, :]], [("d", "out", t, s)])

    try:
        prologue()
        for t in range(NT):
            tile(t)
    except _Stop:
        pass
    dump_specs = []
    for name, v in dumps:
        shp = [128, int(np.prod(v.ap.shape[1:]))]
        dd = nc.dram_tensor("dbg_" + name, list(v.ap.shape), F32, kind="ExternalOutput").ap()
        dump_specs.append(name)
        S.dma("pool", "dbg_" + name, lambda e, dd=dd, v=v: e.dma_start(out=dd, in_=v.ap), [v], [("d", "dbg", name)])

    semkeys = sorted(S.cnt.keys())
    with ExitStack() as es:
        sems = {k: es.enter_context(nc.semaphore("sem_" + k)) for k in semkeys}
        block = es.enter_context(nc.Block())

        def replay(engname, final_waits=()):
            def run(e):
                for waits, fns, sk, inc, every in S.ops[engname]:
                    for k, v in waits:
                        e.wait_ge(sems[k], v)
                    inst = None
                    for fn in fns:
                        inst = fn(e)
                        if every:
                            inst.then_inc(sems[sk], inc)
                    if not every:
                        inst.then_inc(sems[sk], inc)
                for k in final_waits:
                    e.wait_ge(sems[k], S.cnt[k])
            return run

        block.tensor(replay("pe"))
        block.scalar(replay("act"))
        block.vector(replay("dve"))
        block.gpsimd(replay("pool", [k for k in semkeys if k.startswith("o") or k.startswith("dbg_")]))
        block.sync(replay("sp"))
    nc._dump_specs = dump_specs
    return nc


_NC_CACHE = {}


def _col(v, n):
    return np.ascontiguousarray(np.asarray(v, np.float32).reshape(n, 128).T)


def make_shared(meta, g_pre_mix, w_in, b_gates, conf_dw_w, conf_dw_b, conf_ln_g, conf_ln_b, conf_w_pw,
                short_dw_w, short_w_out, w_o, g_post_mix, g_pre_mlp, w_up, w_down, g_post_mlp):
    f = lambda a: np.ascontiguousarray(np.asarray(a, dtype=np.float32))
    cw = np.asarray(conf_dw_w, np.float32)[0]
    cw_col = np.ascontiguousarray(cw.T.reshape(8, 128, 31).transpose(1, 0, 2).reshape(128, 8 * 31))
    sw = np.asarray(short_dw_w, np.float32)[0]
    sw_col = np.ascontiguousarray(sw.T.reshape(8, 128, 3).transpose(1, 0, 2).reshape(128, 8 * 3))
    return {
        "meta": f(meta),
        "w_in": f(w_in[0]),
        "conf_w_pw": f(conf_w_pw[0]),
        "short_w_out": f(short_w_out[0]),
        "w_o": f(w_o[0]),
        "w_up": f(w_up[0]),
        "w_down": f(w_down[0]),
        "gcol_pre_mix": _col(g_pre_mix[0], 16),
        "gcol_pre_mlp": _col(g_pre_mlp[0], 16),
        "g_post_mix": f(g_post_mix[0]).reshape(1, D),
        "g_post_mlp": f(g_post_mlp[0]).reshape(1, D),
        "bgates_col": _col(b_gates[0], 32),
        "conf_w_col": cw_col,
        "conf_b_col": _col(conf_dw_b[0], 8),
        "conf_lng_col": _col(conf_ln_g[0], 8),
        "conf_lnb_col": _col(conf_ln_b[0], 8),
        "short_w_col": sw_col,
        "ident": np.eye(128, dtype=np.float32),
    }


def kernel(x, meta, g_pre_mix, w_in, b_gates, conf_dw_w, conf_dw_b, conf_ln_g, conf_ln_b, conf_w_pw,
           short_dw_w, short_w_out, w_o, g_post_mix, g_pre_mlp, w_up, w_down, g_post_mlp):
    if "nc" not in _NC_CACHE:
        _NC_CACHE["nc"] = build_nc()
    nc = _NC_CACHE["nc"]
    x = np.ascontiguousarray(np.asarray(x, dtype=np.float32))
    shared = make_shared(meta, g_pre_mix, w_in, b_gates, conf_dw_w, conf_dw_b, conf_ln_g, conf_ln_b, conf_w_pw,
                         short_dw_w, short_w_out, w_o, g_post_mix, g_pre_mlp, w_up, w_down, g_post_mlp)
    in_maps = []
    for b in range(N_CORES):
        m = dict(shared)
        m["x"] = x[b]
        in_maps.append(m)
    res = run_bass_kernel_spmd(nc, in_maps, core_ids=list(range(N_CORES)))
    return np.stack([np.asarray(r["out"], dtype=np.float32) for r in res.results], axis=0)
```

```python
import itertools
from contextlib import ExitStack

import numpy as np
import concourse.bass as bass
import concourse.mybir as mybir
from concourse.bass_utils import run_bass_kernel_spmd

F32 = mybir.dt.float32
BF16 = mybir.dt.bfloat16
AF = mybir.ActivationFunctionType
ALU = mybir.AluOpType

D = 2048
SEQ = 2048
NMETA = 16
DC = 1024
DFF = 8192
INC = 9216
TT = 512
NT = SEQ // TT
RMS_EPS = 1e-6
LN_EPS = 1e-5
KIB = 1024
N_CORES = 8


class V:
    __slots__ = ("ap", "blk")

    def __init__(self, ap, blk):
        self.ap = ap
        self.blk = blk


def _ranges(off, dims):
    dims = [(s, c) for s, c in dims if c > 1 and s != 0]
    if not dims:
        return [(off, off + 1)]
    if dims[-1][0] == 1:
        inner = dims[-1][1]
        outer = dims[:-1]
    else:
        inner = 1
        outer = dims
    n = 1
    for _, c in outer:
        n *= c
    if n > 512:
        hi = off + sum(s * (c - 1) for s, c in dims) + 1
        return [(off, hi)]
    out = []
    for idx in itertools.product(*[range(c) for _, c in outer]):
        o = off + sum(i * s for i, (s, _) in zip(idx, outer))
        out.append((o, o + inner))
    return out


class Buf:
    def __init__(self, nc, name, shape, dtype, off):
        self.t = nc.alloc_sbuf_tensor_at(name, list(shape), dtype, offset=off)
        self.ap = self.t.ap()
        self.off = off
        self.esz = 4 if dtype == F32 else 2
        self.pstep = int(np.prod(shape[1:]))
        self.nbytes = self.pstep * self.esz

    def __getitem__(self, idx):
        v = self.ap[idx]
        fo = v.offset % self.pstep
        dims = list(v.ap)[1:]
        blk = []
        for lo, hi in _ranges(fo, dims):
            b0 = (self.off + lo * self.esz) // 64
            b1 = (self.off + hi * self.esz - 1) // 64
            blk.extend(("s", b) for b in range(b0, b1 + 1))
        return V(v, blk)


class Sched:
    ENG = ("pe", "act", "dve", "pool", "sp")

    def __init__(self):
        self.ops = {e: [] for e in self.ENG}
        self.cnt = {}
        self.waited = {}
        self.lastw = {}
        self.readers = {}

    @staticmethod
    def _blks(items):
        for it in items:
            if isinstance(it, V):
                yield from it.blk
            else:
                yield it

    def _deps(self, eng, reads, writes, selfsync):
        deps = {}

        def add(tok):
            if tok is None:
                return
            k, v = tok
            if k == eng and not selfsync:
                return
            if deps.get(k, 0) < v:
                deps[k] = v

        for b in reads:
            add(self.lastw.get(b))
            if b[0] == "p":
                r = self.readers.get(b)
                if r:
                    for k, v in r.items():
                        add((k, v))
        for b in writes:
            add(self.lastw.get(b))
            r = self.readers.get(b)
            if r:
                for k, v in r.items():
                    add((k, v))
        waits = []
        for k, v in deps.items():
            if self.waited.get((eng, k), 0) < v:
                self.waited[(eng, k)] = v
                waits.append((k, v))
        return waits

    def _commit(self, tok, reads, writes):
        k, v = tok
        for b in reads:
            r = self.readers.setdefault(b, {})
            if r.get(k, 0) < v:
                r[k] = v
        for b in writes:
            self.lastw[b] = tok
            self.readers[b] = {}

    def op(self, eng, fns, reads, writes, selfsync=False):
        if callable(fns):
            fns = [fns]
        reads = list(self._blks(reads))
        writes = list(self._blks(writes))
        waits = self._deps(eng, reads, writes, selfsync)
        self.cnt[eng] = self.cnt.get(eng, 0) + 1
        tok = (eng, self.cnt[eng])
        self._commit(tok, reads, writes)
        self.ops[eng].append((waits, fns, eng, 1, False))
        return tok

    def dma(self, eng, semkey, fns, reads, writes):
        if callable(fns):
            fns = [fns]
        reads = list(self._blks(reads))
        writes = list(self._blks(writes))
        waits = self._deps(eng, reads, writes, True)
        self.cnt[semkey] = self.cnt.get(semkey, 0) + 16 * len(fns)
        tok = (semkey, self.cnt[semkey])
        self._commit(tok, reads, writes)
        self.ops[eng].append((waits, fns, semkey, 16, True))
        return tok


class _Stop(Exception):
    pass


def build_nc(stop=None, conv_window=5):
    nc = bass.Bass("TRN2", target_bir_lowering=False)
    dumps = []

    def checkpoint(name, views):
        if stop == name:
            dumps.extend(views)
            raise _Stop()

    def din(name, shape):
        return nc.dram_tensor(name, list(shape), F32, kind="ExternalInput").ap()

    x_d = din("x", [SEQ, D])
    meta_d = din("meta", [NMETA, D])
    w_in_d = din("w_in", [D, INC])
    pw_d = din("conf_w_pw", [DC, D])
    so_d = din("short_w_out", [DC, D])
    wo_d = din("w_o", [D, D])
    up_d = din("w_up", [D, DFF])
    dn_d = din("w_down", [DFF, D])
    gpm_d = din("gcol_pre_mix", [128, 16])
    gpl_d = din("gcol_pre_mlp", [128, 16])
    gpost_mix_d = din("g_post_mix", [1, D])
    gpost_mlp_d = din("g_post_mlp", [1, D])
    bgc_d = din("bgates_col", [128, 32])
    cw_d = din("conf_w_col", [128, 8 * 31])
    cb_d = din("conf_b_col", [128, 8])
    lng_d = din("conf_lng_col", [128, 8])
    lnb_d = din("conf_lnb_col", [128, 8])
    sw_d = din("short_w_col", [128, 8 * 3])
    ident_d = din("ident", [128, 128])
    out_d = nc.dram_tensor("out", [SEQ, D], F32, kind="ExternalOutput").ap()

    N_UNITS = 58
    scr = nc.dram_tensor("scr", [N_UNITS, 128, 16, 512], BF16, kind="Internal").ap()

    def B(name, shape, dtype, off):
        return Buf(nc, name, shape, dtype, off)

    wbuf = [B("wbuf0", [128, 16, 512], BF16, 16 * KIB), B("wbuf1", [128, 16, 512], BF16, 32 * KIB)]
    h = B("h", [128, 4, 2048], F32, 48 * KIB)
    gbc = B("gbc", [128, 2048], F32, 80 * KIB)
    cbuf = B("c", [128, 8, 512], F32, 88 * KIB)
    nT = B("nT", [128, 16, 512], BF16, 104 * KIB)
    park3 = B("park3", [128, 4, 2048], F32, 88 * KIB)
    R2 = 120 * KIB
    fT = B("fT", [128, 64, 512], BF16, R2)
    mT = B("mT", [128, 16, 512], BF16, R2)
    o = R2 + 16 * KIB
    a_bf = B("a_bf", [128, 8, 544], BF16, o)
    o += 8 * 544 * 2
    cv_bf = B("cv_bf", [128, 8, 520], BF16, o)
    o += 8 * 520 * 2
    aln = B("aln", [128, 8, 512], BF16, o)
    o += 8192
    s_bf = B("s_bf", [128, 8, 512], BF16, o)
    o += 8192
    park1 = B("park1", [128, 4, 2048], F32, R2 + 16 * KIB)
    assert o % 64 == 0 and o >= R2 + 48 * KIB
    parkB = B("parkB", [128, 4, 512], F32, o)
    o += 8192
    cbf_t = [B("cbf0", [128, 512], BF16, o), B("cbf1", [128, 512], BF16, o + 1024)]
    csq_t = [B("csq0", [128, 512], BF16, o + 2048), B("csq1", [128, 512], BF16, o + 3072)]
    o += 4096
    assert o <= R2 + 64 * KIB
    xs = B("xs", [128, 2048], F32, 184 * KIB)
    parkA = B("parkA", [128, 4, 512], F32, 184 * KIB)
    xs2 = B("xs2", [128, 2048], F32, parkB.off)
    xs_bufs = [xs, xs2]
    dbuf = [B("dbuf0", [128, 31, 128], BF16, 192 * KIB), B("dbuf1", [128, 31, 128], BF16, 192 * KIB + 7936)]
    o = 192 * KIB + 2 * 7936
    assert o % 64 == 0

    def small(name, shape, dtype):
        nonlocal o
        b = B(name, shape, dtype, o)
        o += (b.nbytes + 63) // 64 * 64
        return b

    lnv = small("lnv", [128, 512], F32)
    junk = B("junk", [128, 1024], BF16, lnv.off)
    tmpA = [small("tmpA0", [128, 512], F32), small("tmpA1", [128, 512], F32)]
    tmpB = [small("tmpB0", [128, 512], F32), small("tmpB1", [128, 512], F32)]
    ident = small("ident", [128, 128], F32)
    ones_bf = small("ones_bf", [128, 128], BF16)
    gpm = small("gpm", [128, 16], F32)
    gpl = small("gpl", [128, 16], F32)
    bgc = small("bgc", [128, 32], F32)
    cwc = small("cwc", [128, 8, 31], F32)
    cbc = small("cbc", [128, 8], F32)
    lngc = small("lngc", [128, 8], F32)
    lnbc = small("lnbc", [128, 8], F32)
    swc = small("swc", [128, 8, 3], F32)
    hist_a = small("hist_a", [128, 8, 32], BF16)
    hist_cv = small("hist_cv", [128, 8, 2], BF16)
    d3buf = [small("d3b0", [128, 3, 128], BF16), small("d3b1", [128, 3, 128], BF16)]
    neghalf = small("neghalf", [128, 1], F32)
    nTm = small("nTm", [128, 16, NMETA], BF16)
    parkAm = small("parkAm", [128, 4, NMETA], F32)
    st = small("st", [128, 64], F32)
    assert o <= 224 * KIB, o

    pst = nc.alloc_psum_tensor("ps", [128, 8, 512], F32).ap()

    def PS(b, lo=0, hi=512, np_=128):
        return V(pst[0:np_, b, lo:hi], [("p", b)])

    S = Sched()
    bank_ctr = [0]

    def alloc_bank():
        b = bank_ctr[0] % 8
        bank_ctr[0] += 1
        return b

    def ACT(fn, r, w, **kw):
        return S.op("act", fn, r, w, **kw)

    def DVE(fn, r, w, **kw):
        return S.op("dve", fn, r, w, **kw)

    def POOL(fn, r, w, **kw):
        return S.op("dve", fn, r, w, **kw)

    def PE(fns, r, w, **kw):
        return S.op("pe", fns, r, w, **kw)

    C_SS = 0
    C_MS = 8
    C_RSTD = 12
    C_SSQ = 16
    C_MS2 = 32
    C_RSTD2 = 36

    cl = []
    for dst, src in ((ident, ident_d), (gpm, gpm_d), (gpl, gpl_d), (bgc, bgc_d), (cbc, cb_d),
                     (lngc, lng_d), (lnbc, lnb_d)):
        cl.append((dst[:, :], src))
    cl.append((cwc[:, :, :], cw_d.rearrange("p (j k) -> p j k", k=31)))
    cl.append((swc[:, :, :], sw_d.rearrange("p (j k) -> p j k", k=3)))
    S.dma("pool", "const", [(lambda e, d=d, s=s: e.dma_start(out=d.ap, in_=s)) for d, s in cl],
          [], [d for d, _ in cl])

    conv_id = [0]
    conv_keys = []

    def issue_conv(n):
        return

    def convert_unit(u, pieces):
        i = conv_id[0]
        conv_id[0] += 1
        key = ("d", "scr", u)
        thr = [conv_keys[i - conv_window]] if i >= conv_window else []
        conv_keys.append(key)
        S.dma("pool", f"cv{i}", [(lambda e, d=d, s=s: e.dma_start(out=d, in_=s)) for d, s in pieces], thr, [key])

    POOL(lambda e: e.memset(gbc.ap, 0.0), [], [gbc[:, :]])
    S.dma("act", "gbc", lambda e: e.dma_start(out=gbc[0:NMETA, :].ap, in_=meta_d), [], [gbc[:, :]])

    POOL(lambda e: e.memset(neghalf.ap, -0.5), [], [neghalf[:, :]])
    POOL(lambda e: e.memset(ones_bf.ap, 1.0 / DC), [], [ones_bf[:, :]])
    POOL(lambda e: e.memset(hist_a.ap, 0.0), [], [hist_a[:, :, :]])
    POOL(lambda e: e.memset(hist_cv.ap, 0.0), [], [hist_cv[:, :, :]])

    slot_ctr = [0]
    win_v = w_in_d.rearrange("(k p) c -> p k c", p=128)
    pw_v = pw_d.rearrange("(k p) c -> p k c", p=128)
    so_v = so_d.rearrange("(k p) c -> p k c", p=128)
    wo_v = wo_d.rearrange("(k p) c -> p k c", p=128)
    up_v = up_d.rearrange("(k p) c -> p k c", p=128)
    dn_v = dn_d.rearrange("(k p) c -> p k c", p=128)
    cur_t = [0]
    unit_ctr = [0]

    def load_unit(parts, keys=None):
        u = unit_ctr[0]
        unit_ctr[0] += 1
        if cur_t[0] == 0:
            convert_unit(u, [(scr[u, :, k0:k1, :], srcv) for k0, k1, srcv in parts])
        slot = slot_ctr[0] % 2
        slot_ctr[0] += 1
        wb = wbuf[slot]
        S.dma("sp", f"w{slot}", lambda e: e.dma_start(out=wb.ap, in_=scr[u]), [("d", "scr", u)], [wb[:, :, :]])
        return slot

    def load_win(col0):
        return load_unit([(0, 16, win_v[:, :, col0:col0 + 512])])

    def rstd_from(np_, ss_cols, ms_col, rstd_col, eps, denom):
        c0, n = ss_cols
        ms = st[0:np_, ms_col:ms_col + 1]
        DVE(lambda e: e.tensor_reduce(out=ms.ap, in_=st.ap[0:np_, c0:c0 + n],
                                      axis=mybir.AxisListType.X, op=ALU.add),
            [st[0:np_, c0:c0 + n]], [ms])
        DVE(lambda e: e.tensor_scalar(out=ms.ap, in0=ms.ap, scalar1=1.0 / denom, scalar2=eps,
                                      op0=ALU.mult, op1=ALU.add), [ms], [ms], selfsync=True)
        rs = st[0:np_, rstd_col:rstd_col + 1]
        ACT(lambda e: e.activation(out=ms.ap, in_=ms.ap, func=AF.Sqrt), [ms], [ms])
        DVE(lambda e: e.reciprocal(out=rs.ap, in_=ms.ap), [ms], [rs])
        return rs

    def norm_T(hrow, np_, s, gcol, col0, dstbuf=None, nout=None):
        dstbuf = nT if dstbuf is None else dstbuf
        nout = np_ if nout is None else nout
        for hh in range(2):
            cc = C_SS + s * 2 + hh
            ACT(lambda e, hh=hh, cc=cc: e.activation(out=junk.ap[0:np_, :], in_=hrow.ap[:, hh * 1024:(hh + 1) * 1024],
                                                     func=AF.Square, accum_out=st.ap[0:np_, cc:cc + 1]),
                [hrow], [junk[0:np_, :], st[0:np_, cc:cc + 1]])
        checkpoint("nt1", [("st", st[:, :])])
        rs = rstd_from(np_, (C_SS + s * 2, 2), C_MS + s, C_RSTD + s, RMS_EPS, float(D))
        checkpoint("nt2", [("st", st[:, :])])
        xsb = xs_bufs[s % 2]
        xv = xsb[0:np_, :]
        ACT(lambda e: e.activation(out=xv.ap, in_=hrow.ap, func=AF.Identity, scale=rs.ap), [hrow, rs], [xv])
        checkpoint("nt3", [("xs", xsb[:, :])])
        for kb in range(4):
            b = alloc_bank()
            fns = [(lambda e, k=kb * 4 + i, i=i, b=b: e.transpose(out=pst[:, b, i * 128:i * 128 + np_],
                                                            in_=xsb.ap[0:np_, k * 128:(k + 1) * 128],
                                                            identity=ident.ap[0:np_, 0:np_]))
                   for i in range(4)]
            PE(fns, [xv, ident[:, :]], [("p", b)])
            checkpoint("nt4", [("st", st[:, :])])
            for i in range(4):
                k = kb * 4 + i
                dst = dstbuf[:, k, col0:col0 + nout]
                src = PS(b, i * 128, i * 128 + nout)
                if kb % 2 == 0:
                    ACT(lambda e, dst=dst, src=src, k=k: e.activation(out=dst.ap, in_=src.ap, func=AF.Identity,
                                                                      scale=gcol.ap[:, k:k + 1]),
                        [src, gcol[:, k:k + 1]], [dst])
                    checkpoint("nt5", [("st", st[:, :])])
                else:
                    DVE(lambda e, dst=dst, src=src, k=k: e.tensor_scalar(out=dst.ap, in0=src.ap,
                                                                         scalar1=gcol.ap[:, k:k + 1], scalar2=None,
                                                                         op0=ALU.mult),
                        [src, gcol[:, k:k + 1]], [dst])

    def formF_chunk(slot, k0, kn, jj, rhs_buf, rk0, ncols):
        b = alloc_bank()
        wb = wbuf[slot]
        fns = [(lambda e, k=k: e.matmul(out=pst[:, b, 0:ncols], lhsT=wb.ap[:, k0 + k, jj * 128:(jj + 1) * 128],
                                        rhs=rhs_buf.ap[:, rk0 + k, 0:ncols], start=(k == 0), stop=(k == kn - 1)))
               for k in range(kn)]
        PE(fns, [wb[:, k0:k0 + kn, jj * 128:(jj + 1) * 128], rhs_buf[:, rk0:rk0 + kn, 0:ncols]], [("p", b)])
        return b

    def build_d3(j):
        d3 = d3buf[j % 2]
        POOL(lambda e: e.tensor_tensor(out=d3.ap, in0=ident.ap.unsqueeze(1).broadcast_to([128, 3, 128]),
                                       in1=swc.ap[:, j, :].unsqueeze(2).broadcast_to([128, 3, 128]), op=ALU.mult),
             [ident[:, :], swc[:, j, :]], [d3[:, :, :]])

    def build_d31(j):
        dd = dbuf[j % 2]
        POOL(lambda e: e.tensor_tensor(out=dd.ap, in0=ident.ap.unsqueeze(1).broadcast_to([128, 31, 128]),
                                       in1=cwc.ap[:, j, :].unsqueeze(2).broadcast_to([128, 31, 128]), op=ALU.mult),
             [ident[:, :], cwc[:, j, :]], [dd[:, :, :]])

    def win_part1(with_meta):
        def meta_chunk(slot, jj):
            b = formF_chunk(slot, 0, 16, jj, nTm, 0, NMETA)
            return PS(b, 0, NMETA)

        for half in range(2):
            slot = load_win(1024 + half * 512)
            for jj in range(4):
                b = formF_chunk(slot, 0, 16, jj, nT, 0, TT)
                src = PS(b)
                dst = parkA[:, jj, :]
                ACT(lambda e, dst=dst, src=src: e.activation(out=dst.ap, in_=src.ap, func=AF.Sigmoid), [src], [dst])
                if with_meta:
                    src = meta_chunk(slot, jj)
                    dst = parkAm[:, jj, :]
                    ACT(lambda e, dst=dst, src=src: e.activation(out=dst.ap, in_=src.ap, func=AF.Sigmoid), [src], [dst])
            slot = load_win(0 + half * 512)
            for jj in range(4):
                j = half * 4 + jj
                b = formF_chunk(slot, 0, 16, jj, nT, 0, TT)
                src = PS(b)
                pk = parkA[:, jj, :]
                dst = a_bf[:, j, 30:542]
                DVE(lambda e, dst=dst, src=src, pk=pk: e.tensor_tensor(out=dst.ap, in0=src.ap, in1=pk.ap, op=ALU.mult),
                    [src, pk], [dst])
                if with_meta:
                    src = meta_chunk(slot, jj)
                    pk = parkAm[:, jj, :]
                    dst = a_bf[:, j, 14:30]
                    DVE(lambda e, dst=dst, src=src, pk=pk: e.tensor_tensor(out=dst.ap, in0=src.ap, in1=pk.ap,
                                                                           op=ALU.mult), [src, pk], [dst])
        for half in range(2):
            slot = load_win(2048 + half * 512)
            for jj in range(4):
                b = formF_chunk(slot, 0, 16, jj, nT, 0, TT)
                src = PS(b)
                dst = parkB[:, jj, :]
                ACT(lambda e, dst=dst, src=src: e.activation(out=dst.ap, in_=src.ap, func=AF.Copy), [src], [dst])
            slot = load_win(3072 + half * 512)
            for jj in range(4):
                b = formF_chunk(slot, 0, 16, jj, nT, 0, TT)
                src = PS(b)
                dst = parkA[:, jj, :]
                ACT(lambda e, dst=dst, src=src: e.activation(out=dst.ap, in_=src.ap, func=AF.Copy), [src], [dst])
                if with_meta:
                    src = meta_chunk(slot, jj)
                    dst = parkAm[:, jj, :]
                    ACT(lambda e, dst=dst, src=src: e.activation(out=dst.ap, in_=src.ap, func=AF.Copy), [src], [dst])
            slot = load_win(4096 + half * 512)
            for jj in range(4):
                j = half * 4 + jj
                b = formF_chunk(slot, 0, 16, jj, nT, 0, TT)
                src = PS(b)
                pk = parkA[:, jj, :]
                dst = cv_bf[:, j, 2:514]
                DVE(lambda e, dst=dst, src=src, pk=pk: e.tensor_tensor(out=dst.ap, in0=src.ap, in1=pk.ap, op=ALU.mult),
                    [src, pk], [dst])
                if with_meta:
                    b = formF_chunk(slot, 0, 16, jj, nTm, 0, NMETA)
                    src = PS(b, 14, 16)
                    pk = parkAm[:, jj, 14:16]
                    dst = cv_bf[:, j, 0:2]
                    DVE(lambda e, dst=dst, src=src, pk=pk: e.tensor_tensor(out=dst.ap, in0=src.ap, in1=pk.ap,
                                                                           op=ALU.mult), [src, pk], [dst])
            for jj in range(4):
                j = half * 4 + jj
                build_d3(j)
                d3 = d3buf[j % 2]
                b = alloc_bank()
                fns = [(lambda e, k=k, j=j, d3=d3, b=b: e.matmul(out=pst[:, b, :], lhsT=d3.ap[:, k, :],
                                                             rhs=cv_bf.ap[:, j, k:k + 512],
                                                             start=(k == 0), stop=(k == 2))) for k in range(3)]
                PE(fns, [d3[:, :, :], cv_bf[:, j, 0:514]], [("p", b)])
                src = PS(b)
                pk = parkB[:, jj, :]
                dst = s_bf[:, j, :]
                DVE(lambda e, dst=dst, src=src, pk=pk: e.tensor_tensor(out=dst.ap, in0=src.ap, in1=pk.ap,
                                                                       op=ALU.mult), [src, pk], [dst])

    def prologue():
        checkpoint("init", [("cwc", cwc[:, :, :]), ("ones", ones_bf[:, :])])
        norm_T(gbc[:, :], 128, 0, gpm, 0, dstbuf=nTm, nout=NMETA)
        checkpoint("pro_norm", [("nT", nTm[:, :, :]), ("st", st[:, :])])

    def tile(t):
        cur_t[0] = t
        unit_ctr[0] = 0
        for s in range(4):
            r0 = t * TT + s * 128
            S.dma("act", f"x{s}", lambda e, s=s, r0=r0: e.dma_start(out=h.ap[:, s, :], in_=x_d[r0:r0 + 128, :]),
                  [], [h[:, s, :]])
        if t == 0:
            POOL(lambda e: e.memset(a_bf.ap[:, :, 0:14], 0.0), [], [a_bf[:, :, 0:14]])
        else:
            POOL(lambda e: e.tensor_copy(out=a_bf.ap[:, :, 0:30], in_=hist_a.ap[:, :, 0:30]),
                 [hist_a[:, :, 0:30]], [a_bf[:, :, 0:30]])
            POOL(lambda e: e.tensor_copy(out=cv_bf.ap[:, :, 0:2], in_=hist_cv.ap[:, :, 0:2]),
                 [hist_cv[:, :, :]], [cv_bf[:, :, 0:2]])
        for s in range(4):
            norm_T(h[:, s, :], 128, s, gpm, s * 128)

        issue_conv(2)
        if t == 0:
            checkpoint("t0_norm", [("nT", nT[:, :, :])])
        win_part1(t == 0)
        issue_conv(2)
        if t == 0:
            checkpoint("t0_win1", [("a_bf", a_bf[:, :, :]), ("cv_bf", cv_bf[:, :, :]), ("s_bf", s_bf[:, :, :])])

        POOL(lambda e: e.tensor_copy(out=hist_a.ap[:, :, 0:30], in_=a_bf.ap[:, :, 512:542]),
             [a_bf[:, :, 512:542]], [hist_a[:, :, 0:30]], selfsync=True)
        POOL(lambda e: e.tensor_copy(out=hist_cv.ap[:, :, 0:2], in_=cv_bf.ap[:, :, 512:514]),
             [cv_bf[:, :, 512:514]], [hist_cv[:, :, :]], selfsync=True)

        S1 = alloc_bank()
        S2 = alloc_bank()
        build_d31(0)
        build_d31(1)

        def stats_mm(j):
            cb_, cs_ = cbf_t[j % 2], csq_t[j % 2]
            PE(lambda e: e.matmul(out=pst[:, S1, :], lhsT=ones_bf.ap, rhs=cb_.ap, start=(j == 0), stop=(j == 7)),
               [ones_bf[:, :], cb_[:, :]], [("p", S1)])
            PE(lambda e: e.matmul(out=pst[:, S2, :], lhsT=ones_bf.ap, rhs=cs_.ap, start=(j == 0), stop=(j == 7)),
               [ones_bf[:, :], cs_[:, :]], [("p", S2)])

        for j in range(8):
            dd = dbuf[j % 2]
            b = alloc_bank()
            while b in (S1, S2):
                b = alloc_bank()
            fns = [(lambda e, k=k, j=j, dd=dd, b=b: e.matmul(out=pst[:, b, :], lhsT=dd.ap[:, k, :],
                                                             rhs=a_bf.ap[:, j, k:k + 512],
                                                             start=(k == 0), stop=(k == 30))) for k in range(31)]
            PE(fns, [dd[:, :, :], a_bf[:, j, 0:542]], [("p", b)])
            if j + 2 < 8:
                build_d31(j + 2)
            src = PS(b)
            cbj = cbc[:, j:j + 1]
            ACT(lambda e, src=src, j=j, cbj=cbj: e.activation(out=cbuf.ap[:, j, :], in_=src.ap, func=AF.Identity,
                                                              bias=cbj.ap), [src, cbj], [cbuf[:, j, :]])
            cb_, cs_ = cbf_t[j % 2], csq_t[j % 2]
            DVE(lambda e, j=j, cb_=cb_: e.tensor_copy(out=cb_.ap, in_=cbuf.ap[:, j, :]),
                [cbuf[:, j, :]], [cb_[:, :]])
            ACT(lambda e, src=src, cbj=cbj, cs_=cs_: e.activation(out=cs_.ap, in_=src.ap, func=AF.Square,
                                                                  bias=cbj.ap), [src, cbj], [cs_[:, :]])
            if j >= 1:
                stats_mm(j - 1)
        stats_mm(7)

        s1v, s2v = PS(S1), PS(S2)
        ACT(lambda e: e.activation(out=lnv.ap, in_=s1v.ap, func=AF.Square), [s1v], [lnv[:, :]])
        DVE(lambda e: e.tensor_tensor(out=lnv.ap, in0=s2v.ap, in1=lnv.ap, op=ALU.subtract),
            [s2v, lnv[:, :]], [lnv[:, :]])
        DVE(lambda e: e.tensor_scalar(out=lnv.ap, in0=lnv.ap, scalar1=LN_EPS, scalar2=0.0, op0=ALU.add, op1=ALU.max),
            [lnv[:, :]], [lnv[:, :]], selfsync=True)
        ACT(lambda e: e.activation(out=lnv.ap, in_=lnv.ap, func=AF.Sqrt), [lnv[:, :]], [lnv[:, :]])
        DVE(lambda e: e.reciprocal(out=lnv.ap, in_=lnv.ap), [lnv[:, :]], [lnv[:, :]])

        def ln_apply(j):
            ta, tb = tmpA[j % 2], tmpB[j % 2]
            DVE(lambda e: e.tensor_tensor(out=ta.ap, in0=cbuf.ap[:, j, :], in1=s1v.ap, op=ALU.subtract),
                [cbuf[:, j, :], s1v], [ta[:, :]])
            DVE(lambda e: e.tensor_tensor(out=tb.ap, in0=ta.ap, in1=lnv.ap, op=ALU.mult),
                [ta[:, :], lnv[:, :]], [tb[:, :]], selfsync=True)
            ACT(lambda e: e.activation(out=tb.ap, in_=tb.ap, func=AF.Identity, scale=lngc.ap[:, j:j + 1],
                                       bias=lnbc.ap[:, j:j + 1]),
                [tb[:, :], lngc[:, j:j + 1], lnbc[:, j:j + 1]], [tb[:, :]])
            ACT(lambda e: e.activation(out=ta.ap, in_=tb.ap, func=AF.Sigmoid), [tb[:, :]], [ta[:, :]], selfsync=True)
            POOL(lambda e: e.tensor_tensor(out=aln.ap[:, j, :], in0=tb.ap, in1=ta.ap, op=ALU.mult),
                 [ta[:, :], tb[:, :]], [aln[:, j, :]])

        for j in range(8):
            ln_apply(j)

        issue_conv(2)
        if t == 0:
            checkpoint("t0_conv", [("c", cbuf[:, :, :]), ("aln", aln[:, :, :]), ("lnv", lnv[:, :])])
        for q in range(4):
            slot = load_win(5120 + q * 512)
            for jj in range(4):
                b = formF_chunk(slot, 0, 16, jj, nT, 0, TT)
                src = PS(b)
                col = bgc[:, q * 4 + jj:q * 4 + jj + 1]
                dst = parkA[:, jj, :]
                ACT(lambda e, dst=dst, src=src, col=col: e.activation(out=dst.ap, in_=src.ap, func=AF.Sigmoid,
                                                                      bias=col.ap), [src, col], [dst])
            slot = load_win(7168 + q * 512)
            for jj in range(4):
                b = formF_chunk(slot, 0, 16, jj, nT, 0, TT)
                src = PS(b)
                col = bgc[:, 16 + q * 4 + jj:16 + q * 4 + jj + 1]
                dst = parkB[:, jj, :]
                ACT(lambda e, dst=dst, src=src, col=col: e.activation(out=dst.ap, in_=src.ap, func=AF.Sigmoid,
                                                                      bias=col.ap), [src, col], [dst])
            slot = load_unit([(0, 8, pw_v[:, :, q * 512:(q + 1) * 512]), (8, 16, so_v[:, :, q * 512:(q + 1) * 512])],
                             [("d", "pw"), ("d", "so")])
            for jj in range(4):
                b = formF_chunk(slot, 0, 8, jj, aln, 0, TT)
                src = PS(b)
                pk = parkA[:, jj, :]
                DVE(lambda e, src=src, pk=pk: e.tensor_tensor(out=pk.ap, in0=src.ap, in1=pk.ap, op=ALU.mult),
                    [src, pk], [pk])
            for jj in range(4):
                b = formF_chunk(slot, 8, 8, jj, s_bf, 0, TT)
                src = PS(b)
                pk = parkB[:, jj, :]
                DVE(lambda e, src=src, pk=pk: e.tensor_tensor(out=pk.ap, in0=src.ap, in1=pk.ap, op=ALU.mult),
                    [src, pk], [pk])
                pa = parkA[:, jj, :]
                dst = mT[:, q * 4 + jj, :]
                POOL(lambda e, dst=dst, pa=pa, pk=pk: e.tensor_tensor(out=dst.ap, in0=pa.ap, in1=pk.ap, op=ALU.add),
                     [pa, pk], [dst], selfsync=True)

        issue_conv(2)
        if t == 0:
            checkpoint("t0_m", [("mT", mT[:, :, :])])
        def formT_stage(act_buf, nk_groups, wview, key_fn, gpost_d, park):
            S.dma("sp", "gbc", lambda e: e.dma_start(out=gbc.ap, in_=gpost_d.broadcast_to([128, D])),
                  [], [gbc[:, :]])
            for j in range(4):
                banks = [alloc_bank() for _ in range(4)]
                for kg in range(nk_groups):
                    slot = load_unit([(0, 16, wview[:, kg * 16:(kg + 1) * 16, j * 512:(j + 1) * 512])], [key_fn(kg)])
                    wb = wbuf[slot]
                    fns = []
                    for k in range(16):
                        for s in range(4):
                            fns.append(lambda e, k=k, s=s, kg=kg, wb=wb, banks=tuple(banks): e.matmul(
                                out=pst[:, banks[s], :], lhsT=act_buf.ap[:, kg * 16 + k, s * 128:(s + 1) * 128],
                                rhs=wb.ap[:, k, :], start=(kg == 0 and k == 0),
                                stop=(kg == nk_groups - 1 and k == 15)))
                    PE(fns, [wb[:, :, :], act_buf[:, kg * 16:(kg + 1) * 16, :]], [("p", bb) for bb in banks])
                for s in range(4):
                    src = PS(banks[s])
                    cc = C_SSQ + s * 4 + j
                    ACT(lambda e, src=src, cc=cc: e.activation(out=junk.ap[:, 0:512], in_=src.ap, func=AF.Square,
                                                               accum_out=st.ap[:, cc:cc + 1]),
                        [src], [junk[:, 0:512], st[:, cc:cc + 1]])
                    dst = park[:, s, j * 512:(j + 1) * 512]
                    gs = gbc[:, j * 512:(j + 1) * 512]
                    DVE(lambda e, dst=dst, src=src, gs=gs: e.tensor_tensor(out=dst.ap, in0=src.ap, in1=gs.ap,
                                                                           op=ALU.mult), [src, gs], [dst])
            for s in range(4):
                rs = rstd_from(128, (C_SSQ + s * 4, 4), C_MS2 + s, C_RSTD2 + s, RMS_EPS, float(D))
                hv = h[:, s, :]
                pv = park[:, s, :]
                DVE(lambda e, hv=hv, pv=pv, rs=rs: e.scalar_tensor_tensor(out=hv.ap, in0=pv.ap, scalar=rs.ap,
                                                                          in1=hv.ap, op0=ALU.mult, op1=ALU.add),
                    [pv, rs, hv], [hv], selfsync=True)

        formT_stage(mT, 1, wo_v, lambda kg: ("d", "wo"), gpost_mix_d, park1)
        issue_conv(2)
        if t == 0:
            checkpoint("t0_wo", [("h", h[:, :, :])])

        for s in range(4):
            norm_T(h[:, s, :], 128, s, gpl, s * 128)
        issue_conv(4)
        if t == 0:
            checkpoint("t0_n2", [("nT", nT[:, :, :])])
        for u in range(16):
            slot = load_unit([(0, 16, up_v[:, :, u * 512:(u + 1) * 512])], [("d", "up", u // 4)])
            for jj in range(4):
                ff = u * 4 + jj
                b = formF_chunk(slot, 0, 16, jj, nT, 0, TT)
                src = PS(b)
                xf = tmpA[ff % 2]
                ACT(lambda e, src=src, xf=xf: e.activation(out=xf.ap, in_=src.ap, func=AF.Copy), [src], [xf[:, :]])
                dst = fT[:, ff, :]
                DVE(lambda e, dst=dst, src=src, xf=xf: e.scalar_tensor_tensor(out=dst.ap, in0=src.ap, scalar=0.0,
                                                                              in1=xf.ap, op0=ALU.max, op1=ALU.mult),
                    [src, xf[:, :]], [dst])
        if t == 0:
            checkpoint("t0_up", [("fT", fT[:, :, :])])
        formT_stage(fT, 4, dn_v, lambda kg: ("d", "dn", kg), gpost_mlp_d, park3)
        if t == 0:
            checkpoint("t0_end", [("h", h[:, :, :]), ("st", st[:, :]), ("park3", park3[:, :, :])])

        for s in range(4):
            r0 = t * TT + s * 128
            S.dma("pool", f"o{s}", lambda e, s=s, r0=r0: e.dma_start(out=out_d[r0:r0 + 128, :], in_=h.ap[:, s, :]),
                  [h[:, s, :]], [("d", "out", t, s)])

    try:
        prologue()
        for t in range(NT):
            tile(t)
    except _Stop:
        pass
    dump_specs = []
    for name, v in dumps:
        shp = [128, int(np.prod(v.ap.shape[1:]))]
        dd = nc.dram_tensor("dbg_" + name, list(v.ap.shape), F32, kind="ExternalOutput").ap()
        dump_specs.append(name)
        S.dma("pool", "dbg_" + name, lambda e, dd=dd, v=v: e.dma_start(out=dd, in_=v.ap), [v], [("d", "dbg", name)])

    semkeys = sorted(S.cnt.keys())
    with ExitStack() as es:
        sems = {k: es.enter_context(nc.semaphore("sem_" + k)) for k in semkeys}
        block = es.enter_context(nc.Block())

        def replay(engname, final_waits=()):
            def run(e):
                for waits, fns, sk, inc, every in S.ops[engname]:
                    for k, v in waits:
                        e.wait_ge(sems[k], v)
                    inst = None
                    for fn in fns:
                        inst = fn(e)
                        if every:
                            inst.then_inc(sems[sk], inc)
                    if not every:
                        inst.then_inc(sems[sk], inc)
                for k in final_waits:
                    e.wait_ge(sems[k], S.cnt[k])
            return run

        block.tensor(replay("pe"))
        block.scalar(replay("act"))
        block.vector(replay("dve"))
        block.gpsimd(replay("pool", [k for k in semkeys if k.startswith("o") or k.startswith("dbg_")]))
        block.sync(replay("sp"))
    nc._dump_specs = dump_specs
    return nc


_NC_CACHE = {}


def _col(v, n):
    return np.ascontiguousarray(np.asarray(v, np.float32).reshape(n, 128).T)


def make_shared(meta, g_pre_mix, w_in, b_gates, conf_dw_w, conf_dw_b, conf_ln_g, conf_ln_b, conf_w_pw,
                short_dw_w, short_w_out, w_o, g_post_mix, g_pre_mlp, w_up, w_down, g_post_mlp):
    f = lambda a: np.ascontiguousarray(np.asarray(a, dtype=np.float32))
    cw = np.asarray(conf_dw_w, np.float32)[0]
    cw_col = np.ascontiguousarray(cw.T.reshape(8, 128, 31).transpose(1, 0, 2).reshape(128, 8 * 31))
    sw = np.asarray(short_dw_w, np.float32)[0]
    sw_col = np.ascontiguousarray(sw.T.reshape(8, 128, 3).transpose(1, 0, 2).reshape(128, 8 * 3))
    return {
        "meta": f(meta),
        "w_in": f(w_in[0]),
        "conf_w_pw": f(conf_w_pw[0]),
        "short_w_out": f(short_w_out[0]),
        "w_o": f(w_o[0]),
        "w_up": f(w_up[0]),
        "w_down": f(w_down[0]),
        "gcol_pre_mix": _col(g_pre_mix[0], 16),
        "gcol_pre_mlp": _col(g_pre_mlp[0], 16),
        "g_post_mix": f(g_post_mix[0]).reshape(1, D),
        "g_post_mlp": f(g_post_mlp[0]).reshape(1, D),
        "bgates_col": _col(b_gates[0], 32),
        "conf_w_col": cw_col,
        "conf_b_col": _col(conf_dw_b[0], 8),
        "conf_lng_col": _col(conf_ln_g[0], 8),
        "conf_lnb_col": _col(conf_ln_b[0], 8),
        "short_w_col": sw_col,
        "ident": np.eye(128, dtype=np.float32),
    }


def kernel(x, meta, g_pre_mix, w_in, b_gates, conf_dw_w, conf_dw_b, conf_ln_g, conf_ln_b, conf_w_pw,
           short_dw_w, short_w_out, w_o, g_post_mix, g_pre_mlp, w_up, w_down, g_post_mlp):
    if "nc" not in _NC_CACHE:
        _NC_CACHE["nc"] = build_nc()
    nc = _NC_CACHE["nc"]
    x = np.ascontiguousarray(np.asarray(x, dtype=np.float32))
    shared = make_shared(meta, g_pre_mix, w_in, b_gates, conf_dw_w, conf_dw_b, conf_ln_g, conf_ln_b, conf_w_pw,
                         short_dw_w, short_w_out, w_o, g_post_mix, g_pre_mlp, w_up, w_down, g_post_mlp)
    in_maps = []
    for b in range(N_CORES):
        m = dict(shared)
        m["x"] = x[b]
        in_maps.append(m)
    res = run_bass_kernel_spmd(nc, in_maps, core_ids=list(range(N_CORES)))
    return np.stack([np.asarray(r["out"], dtype=np.float32) for r in res.results], axis=0)
```
